# Optimizing a Trainium2 kernel written in Bass

```python
import jax, jax.numpy as jnp
from jax import lax
import numpy as np

D_MODEL = 1024
BATCH = 8
SEQ = 2048
DEPTH = 4
DEC_BATCH = 128
DEC_SEQ = 1
PAST_LEN = 16384
PAGE_SIZE = 128

WA = D_MODEL
WB = D_MODEL
KCONV = 31
CHUNK = 128
N_GROUPS_B = 8
HD_B = WB // N_GROUPS_B
PLE_DIM = 256
EPS = 1e-6
SPLITS = (2 * WA, WA, WB, WB, WB, D_MODEL, D_MODEL)
N_IN = sum(SPLITS)

kernel_name = "hybrid_conv_gmlp_gated_decoder_step"


def rms_norm(x, g):
    xf = x.astype(jnp.float32)
    y = xf * lax.rsqrt(jnp.mean(xf * xf, axis=-1, keepdims=True) + EPS)
    return (y * g.astype(jnp.float32)).astype(x.dtype)


def layer_norm(x, g, b):
    xf = x.astype(jnp.float32)
    mu = jnp.mean(xf, axis=-1, keepdims=True)
    var = jnp.mean(jnp.square(xf - mu), axis=-1, keepdims=True)
    y = (xf - mu) * lax.rsqrt(var + EPS)
    return (y * g.astype(jnp.float32) + b.astype(jnp.float32)).astype(x.dtype)


def causal_depthwise_conv(buf, xa, w, b):
    xx = jnp.concatenate([buf.astype(xa.dtype), xa], axis=1)
    y = lax.conv_general_dilated(
        xx, w[:, None, :].astype(xa.dtype), window_strides=(1,), padding='VALID',
        dimension_numbers=('NWC', 'WIO', 'NWC'), feature_group_count=WA)
    return y + b, xx[:, -(KCONV - 1):]


def chunk_spatial_mix(v, w_s, b_s):
    n, t, _ = v.shape
    n_ch = -(-t // CHUNK)
    tp = n_ch * CHUNK
    vp = jnp.pad(v, ((0, 0), (0, tp - t), (0, 0)))
    vr = vp.reshape(n, n_ch, CHUNK, N_GROUPS_B, HD_B)
    mask = jnp.tril(jnp.ones((CHUNK, CHUNK), dtype=bool))
    wm = jnp.where(mask[None], w_s, jnp.zeros_like(w_s))
    s = jnp.einsum('gts,ncsgd->nctgd', wm, vr) + b_s.T[:, :, None]
    return s.reshape(n, tp, WB)[:, :t]


def trunk_layer(x, p_i, buf, norm_g, w_in, conv_w, conv_b, ln_a_g, ln_a_b, w_proj_a,
                ln_v_g, ln_v_b, w_spatial, b_spatial, w_proj_b, w_out,
                ple_norm_g, w_ple_gate, b_ple_gate, w_ple):
    h = rms_norm(x, norm_g)
    z = h @ w_in
    idx = np.cumsum(SPLITS)[:-1].tolist()
    glu_in, za, u, v, zb, ga, gb = jnp.split(z, idx, axis=-1)
    a_lin, a_gate = jnp.split(glu_in, 2, axis=-1)
    xa = a_lin * jax.nn.sigmoid(a_gate)
    ya, new_buf = causal_depthwise_conv(buf, xa, conv_w, conv_b)
    ya = jax.nn.silu(layer_norm(ya, ln_a_g, ln_a_b)) * jax.nn.silu(za)
    ya = ya @ w_proj_a
    vn = layer_norm(v, ln_v_g, ln_v_b)
    s = chunk_spatial_mix(vn, w_spatial, b_spatial)
    yb = (u * s * jax.nn.silu(zb)) @ w_proj_b
    m = jax.nn.sigmoid(ga) * ya + jax.nn.sigmoid(gb) * yb
    x = x + m @ w_out
    gp = jax.nn.sigmoid(rms_norm(x, ple_norm_g) @ w_ple_gate + b_ple_gate)
    x = x + gp * (p_i @ w_ple)
    return x, new_buf, vn


def setup_inputs(seed: int = 0) -> dict:
    key = jax.random.key(seed)
    ks = jax.random.split(key, 24)
    f32 = jnp.float32

    def nrm(k, shape, scale):
        return jax.random.normal(k, shape, f32) * scale

    return {
        "x_prompt": nrm(ks[0], (BATCH, SEQ, D_MODEL), 1.0),
        "x_sample": nrm(ks[1], (DEC_BATCH, DEC_SEQ, D_MODEL), 1.0),
        "state_conv": nrm(ks[2], (DEPTH, DEC_BATCH, KCONV - 1, WA), 0.5),
        "p_prompt": nrm(ks[3], (DEPTH, BATCH, SEQ, PLE_DIM), 1.0),
        "p_sample": nrm(ks[4], (DEPTH, DEC_BATCH, DEC_SEQ, PLE_DIM), 1.0),
        "norm_g": 1.0 + nrm(ks[5], (DEPTH, D_MODEL), 0.02),
        "w_in": nrm(ks[6], (DEPTH, D_MODEL, N_IN), D_MODEL ** -0.5),
        "conv_w": nrm(ks[7], (DEPTH, KCONV, WA), KCONV ** -0.5),
        "conv_b": nrm(ks[8], (DEPTH, WA), 0.02),
        "ln_a_g": 1.0 + nrm(ks[9], (DEPTH, WA), 0.02),
        "ln_a_b": nrm(ks[10], (DEPTH, WA), 0.02),
        "w_proj_a": nrm(ks[11], (DEPTH, WA, D_MODEL), WA ** -0.5),
        "ln_v_g": 1.0 + nrm(ks[12], (DEPTH, WB), 0.02),
        "ln_v_b": nrm(ks[13], (DEPTH, WB), 0.02),
        "w_spatial": nrm(ks[14], (DEPTH, N_GROUPS_B, CHUNK, CHUNK), CHUNK ** -0.5),
        "b_spatial": 1.0 + nrm(ks[15], (DEPTH, N_GROUPS_B, CHUNK), 0.02),
        "w_proj_b": nrm(ks[16], (DEPTH, WB, D_MODEL), WB ** -0.5),
        "w_out": nrm(ks[17], (DEPTH, D_MODEL, D_MODEL), D_MODEL ** -0.5),
        "ple_norm_g": 1.0 + nrm(ks[18], (DEPTH, D_MODEL), 0.02),
        "w_ple_gate": nrm(ks[19], (DEPTH, D_MODEL, D_MODEL), D_MODEL ** -0.5),
        "b_ple_gate": nrm(ks[20], (DEPTH, D_MODEL), 0.02),
        "w_ple": nrm(ks[21], (DEPTH, PLE_DIM, D_MODEL), PLE_DIM ** -0.5),
        "final_g": 1.0 + nrm(ks[22], (D_MODEL,), 0.02),
    }


def reference(x_prompt, x_sample, state_conv, p_prompt, p_sample, norm_g, w_in, conv_w,
              conv_b, ln_a_g, ln_a_b, w_proj_a, ln_v_g, ln_v_b, w_spatial, b_spatial,
              w_proj_b, w_out, ple_norm_g, w_ple_gate, b_ple_gate, w_ple, final_g):
    xp, xs = x_prompt, x_sample
    conv_p, conv_s, vrow_p, vrow_s = [], [], [], []
    zero_buf = jnp.zeros((xp.shape[0], KCONV - 1, WA), xp.dtype)
    for i in range(DEPTH):
        prm = (norm_g[i], w_in[i], conv_w[i], conv_b[i], ln_a_g[i], ln_a_b[i], w_proj_a[i],
               ln_v_g[i], ln_v_b[i], w_spatial[i], b_spatial[i], w_proj_b[i], w_out[i],
               ple_norm_g[i], w_ple_gate[i], b_ple_gate[i], w_ple[i])
        xp, bp, vp = trunk_layer(xp, p_prompt[i], zero_buf, *prm)
        xs, bs, vs = trunk_layer(xs, p_sample[i], state_conv[i], *prm)
        conv_p.append(bp)
        conv_s.append(bs)
        vrow_p.append(vp[:, -CHUNK:])
        vrow_s.append(vs)
    y_prompt = rms_norm(xp, final_g)
    y_sample = rms_norm(xs, final_g)
    return (y_prompt, y_sample, jnp.stack(conv_p), jnp.stack(conv_s),
            jnp.stack(vrow_p), jnp.stack(vrow_s))
```

```python
import numpy as np
from contextlib import ExitStack
import concourse.bass as bass
import concourse.mybir as mybir
from concourse.bass_utils import run_bass_kernel_spmd

F32 = mybir.dt.float32
BF16 = mybir.dt.bfloat16
AF = mybir.ActivationFunctionType
ALU = mybir.AluOpType

D = 1024
DEPTH = 4
SEQ = 2048
NS = 16
KC = 31
PLE = 256
EPS = 1e-6
TT = 512
NCORES = 8
NUNITS = 25
NSLOT = 4
UW = 4096

UNIT_SRC = [
    ("w_in", 0), ("w_in", 1024), ("w_in", 512), ("w_in", 1536),
    ("w_in", 4096), ("w_in", 4608),
    ("w_in", 2048), ("w_in", 2560),
    ("w_in", 3072), ("w_in", 5120), ("w_in", 3584), ("w_in", 5632),
    ("w_proj_a", 0), ("w_in", 6144), ("w_proj_a", 512), ("w_in", 6656),
    ("w_proj_b", 0), ("w_in", 7168), ("w_proj_b", 512), ("w_in", 7680),
    ("w_out", 0), ("w_out", 512),
    ("w_ple_gate", 0), ("w_ple_gate", 512),
    ("w_ple", 0),
]


ORDER = [0, 1, 2, 3, 6, 7, 4, 5, 12, 13, 14, 15, 8, 9, 10, 11, 16, 17, 18, 19, 20, 21, 24, 22, 23]
POS = {u: p for p, u in enumerate(ORDER)}


def cv_group(l, u):
    return u if l <= 1 else POS[u] // 5


class Sched:
    def __init__(self, nc, es):
        self.nc = nc
        self.es = es
        self.eng = {"pe": nc.tensor, "act": nc.scalar, "dve": nc.vector, "pool": nc.gpsimd, "sp": nc.sync}
        self.sem = {k: es.enter_context(nc.semaphore("sem_" + k)) for k in self.eng}
        self.cnt = {k: 0 for k in self.eng}
        self.waited = {k: {} for k in self.eng}
        self.last_w = {}
        self.readers = {}
        self.dsem = {}
        self.dcnt = {}
        self.out_events = []

    def _need(self, e, ev, need):
        if ev is None:
            return
        key, val, src = ev
        cur = need.get(key)
        if cur is None or cur[0] < val:
            need[key] = (val, src)

    def _waits(self, e, reads, writes):
        need = {}
        for r in reads:
            ev = self.last_w.get(r)
            if ev is not None:
                if not (ev[2] == e and e == "pe"):
                    self._need(e, ev, need)
        for w in writes:
            ev = self.last_w.get(w)
            if ev is not None and not (ev[2] == e and e == "pe"):
                self._need(e, ev, need)
            for ev in self.readers.get(w, ()):
                if not (ev[2] == e and e == "pe"):
                    self._need(e, ev, need)
        wd = self.waited[e]
        for key, (val, src) in need.items():
            if wd.get(key, 0) >= val:
                continue
            semh = self.sem[key] if key in self.sem else self.dsem[key]
            self.eng[e].wait_ge(semh, val)
            wd[key] = val

    def _record(self, ev, reads, writes):
        for w in writes:
            self.last_w[w] = ev
            self.readers[w] = []
        for r in reads:
            if r in writes:
                continue
            self.readers.setdefault(r, []).append(ev)

    def op(self, e, fn, reads=(), writes=()):
        self._waits(e, reads, writes)
        inst = fn(self.eng[e])
        self.cnt[e] += 1
        inst.then_inc(self.sem[e], 1)
        ev = (e, self.cnt[e], e)
        self._record(ev, reads, writes)
        return ev

    def dma(self, q, key, out, in_, reads=(), writes=(), is_output=False, **kw):
        self._waits(q, reads, writes)
        if key not in self.dsem:
            self.dsem[key] = self.es.enter_context(self.nc.semaphore("d_" + "_".join(str(x) for x in key)))
            self.dcnt[key] = 0
        inst = self.eng[q].dma_start(out=out, in_=in_, **kw)
        self.dcnt[key] += 16
        inst.then_inc(self.dsem[key], 16)
        ev = (key, self.dcnt[key], "dma")
        self._record(ev, reads, writes)
        if is_output:
            self.out_events.append(ev)
        return ev

    def finish(self):
        final = {}
        for key, val, _ in self.out_events:
            final[key] = max(final.get(key, 0), val)
        for key, val in final.items():
            self.nc.sync.wait_ge(self.dsem[key], val)


class Ring:
    def __init__(self, aps, name):
        self.aps = aps
        self.name = name
        self.i = 0
        self.held = set()

    def next(self, hold=False):
        n = len(self.aps)
        for _ in range(n):
            i = self.i
            self.i = (self.i + 1) % n
            if i not in self.held:
                if hold:
                    self.held.add(i)
                return self.aps[i], (self.name, i)
        raise RuntimeError("ring exhausted")

    def release(self, tok):
        self.held.discard(tok[1])


def build_program():
    nc = bass.Bass("TRN2", target_bir_lowering=False)

    def din(name, shape):
        return nc.dram_tensor(name, list(shape), F32, kind="ExternalInput").ap()

    def dout(name, shape):
        return nc.dram_tensor(name, list(shape), F32, kind="ExternalOutput").ap()

    xp = din("xp", [SEQ, D])
    xs = din("xs", [NS, D])
    sc = din("sc", [DEPTH, NS, KC - 1, D])
    pp = din("pp", [DEPTH, SEQ, PLE])
    psm = din("psm", [DEPTH, NS, PLE])
    norm_g = din("norm_g", [DEPTH, D])
    w_in = din("w_in", [DEPTH, D, 8 * D])
    conv_w = din("conv_w", [DEPTH, KC, D])
    conv_b = din("conv_b", [DEPTH, D])
    ln_a_g = din("ln_a_g", [DEPTH, D])
    ln_a_b = din("ln_a_b", [DEPTH, D])
    w_proj_a = din("w_proj_a", [DEPTH, D, D])
    ln_v_g = din("ln_v_g", [DEPTH, D])
    ln_v_b = din("ln_v_b", [DEPTH, D])
    w_spatial = din("w_spatial", [DEPTH, 8, 128, 128])
    b_spatial = din("b_spatial", [DEPTH, 8, 128])
    w_proj_b = din("w_proj_b", [DEPTH, D, D])
    w_out = din("w_out", [DEPTH, D, D])
    ple_norm_g = din("ple_norm_g", [DEPTH, D])
    w_ple_gate = din("w_ple_gate", [DEPTH, D, D])
    b_ple_gate = din("b_ple_gate", [DEPTH, D])
    w_ple = din("w_ple", [DEPTH, PLE, D])
    final_g = din("final_g", [D])
    wsrc = {"w_in": w_in, "w_proj_a": w_proj_a, "w_proj_b": w_proj_b, "w_out": w_out,
            "w_ple_gate": w_ple_gate, "w_ple": w_ple}

    yp = dout("yp", [SEQ, D])
    ys = dout("ys", [NS, D])
    cp = dout("cp", [DEPTH, KC - 1, D])
    cs = dout("cs", [DEPTH, NS, KC - 1, D])
    vp = dout("vp", [DEPTH, 128, D])
    vs = dout("vs", [DEPTH, NS, D])

    wscr = nc.dram_tensor("wscr", [DEPTH, NUNITS, 128, UW], BF16, kind="Internal").ap()

    with ExitStack() as es:
        S = Sched(nc, es)

        def sb(name, shape, dt):
            return es.enter_context(nc.sbuf_tensor(name, list(shape), dt))

        xT = sb("xT", [128, 8, TT], F32)
        hb = sb("hb", [128, 8, TT], BF16)
        A = sb("A", [128, 8, TT], F32)
        pain = sb("pain", [128, 8, TT], BF16)
        pbin = sb("pbin", [128, 8, TT], BF16)
        vnb = sb("vnb", [128, 4, D], BF16)
        tmpv = sb("tmpv", [128, D], F32)
        xab = [sb(f"xab{i}", [128, KC - 1 + TT], BF16) for i in range(2)]
        hist = sb("hist", [128, DEPTH, 8, KC - 1], BF16)
        xaf = sb("xaf", [128, 8, 32], F32)
        diag = [sb(f"diag{i}", [128, KC, 128], BF16) for i in range(2)]
        yab = [sb(f"yab{i}", [128, TT], BF16) for i in range(2)]
        ysq = [sb(f"ysq{i}", [128, TT], BF16) for i in range(2)]
        tmps = [sb(f"tmp{i}", [128, TT], F32) for i in range(6)]
        sts = [sb(f"st{i}", [128, TT], F32) for i in range(4)]
        rAb = sb("rAb", [128, TT], F32)
        nmrb = sb("nmrb", [128, TT], F32)
        pin = sb("pin", [128, 4, PLE], F32)
        pT = sb("pT", [128, 2, TT], BF16)
        ystage = [sb(f"ystage{i}", [128, D], F32) for i in range(2)]
        lnvg = sb("lnvg", [128, D], F32)
        lnvb = sb("lnvb", [128, D], F32)
        wmT = sb("wmT", [128, DEPTH, 8, 128], BF16)
        bsp = sb("bsp", [1, 8 * 128], BF16)
        bsps = sb("bsps", [1, 8, NS], BF16)
        wm00 = sb("wm00", [NS, DEPTH * 8], F32)
        wm00d = sb("wm00d", [NS, DEPTH * 8, NS], BF16)
        cols = sb("cols", [128, 192], F32)
        cwT = sb("cwT", [128, 8, DEPTH * KC], F32)
        ident_f = sb("ident_f", [128, 128], F32)
        ident_b = sb("ident_b", [128, 128], BF16)
        ones_f = sb("ones_f", [128, 128], F32)
        ones_b = sb("ones_b", [128, 128], BF16)
        epsc = sb("epsc", [128, 1], F32)
        epsc2 = sb("epsc2", [128, 2], F32)
        small = sb("small", [128, 64], F32)
        wslots = [sb(f"wslot{i}", [128, UW], BF16) for i in range(NSLOT)]
        psb = [es.enter_context(nc.psum_tensor(f"ps{i}", [128, 512], F32)) for i in range(8)]

        vnf = ystage[0]
        VNF_T = ("ystage", 0)
        cstage = ystage[1]
        CST_T = ("ystage", 1)

        def xxs_j(j):
            return pbin[:, j, NS:NS + KC * NS].rearrange("p (k s) -> p k s", s=NS)

        PS = Ring(psb, "ps")
        TMP = Ring(tmps, "tmp")
        ST = Ring(sts, "st")

        def col(vec, l, j):
            c = vec * 32 + l * 8 + j
            return cols[:, c:c + 1]
        V_NORMG, V_PLEG, V_CONVB, V_LNAG, V_LNAB, V_BPG = range(6)

        pending_cv = [(l, u) for l in range(DEPTH) for u in ORDER]
        cv_emitted = set()

        def emit_cv(l, u):
            name, c0 = UNIT_SRC[u]
            src = wsrc[name]
            grp = cv_group(l, u)
            if name == "w_ple":
                in_ = src[l].rearrange("(k p) c -> p k c", p=128)
                out = wscr[l, u][:, 0:2 * D].rearrange("p (k c) -> p k c", k=2)
            else:
                in_ = src[l].rearrange("(k p) c -> p k c", p=128)[:, :, c0:c0 + 512]
                out = wscr[l, u].rearrange("p (k c) -> p k c", k=8)
            ev = S.dma("pool", ("cv", l, grp), out, in_)
            S.last_w[("scr", l, grp)] = ev
            cv_emitted.add((l, u))

        def pump(n):
            for _ in range(n):
                if pending_cv:
                    emit_cv(*pending_cv.pop(0))

        def ensure_cv(l, u):
            while (l, u) not in cv_emitted:
                pump(1)

        tile_layers = [(t, l) for t in range(4) for l in range(DEPTH)]
        stream = [(tl, u) for tl in range(len(tile_layers)) for u in ORDER]
        state = {"emitted": 0}
        done_set = set()

        def emit_stream():
            while state["emitted"] < len(stream) and (state["emitted"] < NSLOT or (state["emitted"] - NSLOT) in done_set):
                g = state["emitted"]
                tl, u = stream[g]
                l = tile_layers[tl][1]
                for uu in range(NUNITS):
                    if cv_group(l, uu) == cv_group(l, u):
                        ensure_cv(l, uu)
                slot = g % NSLOT
                width = 2 * D if u == NUNITS - 1 else UW
                S.dma("sp", ("ws", slot), wslots[slot][:, 0:width], wscr[l, u][:, 0:width],
                      reads=[("scr", l, cv_group(l, u))], writes=[("wslot", slot)])
                state["emitted"] += 1

        def unit(tl, u):
            g = tl * NUNITS + POS[u]
            assert g < state["emitted"], (tl, u)
            slot = g % NSLOT
            return wslots[slot], ("wslot", slot)

        def done(tl, *us):
            for u in us:
                done_set.add(tl * NUNITS + POS[u])
            emit_stream()

        def mm_group(out_ap, pairs, reads, bank_tok, first_start=True, last_stop=True, per_reads=None):
            def fn(eng):
                inst = None
                n = len(pairs)
                for i, (lt, rh) in enumerate(pairs):
                    if per_reads is not None:
                        S._waits("pe", per_reads[i], [])
                    inst = eng.matmul(out_ap, lhsT=lt, rhs=rh,
                                      start=(first_start and i == 0), stop=(last_stop and i == n - 1))
                return inst
            ev = S.op("pe", fn, reads=reads, writes=[bank_tok])
            if per_reads is not None:
                for rr in per_reads:
                    for r in rr:
                        S.readers.setdefault(r, []).append(ev)
            return ev

        HBK = [[("hb", k)] for k in range(8)]

        def transposes(items, reads, bank_toks):
            def fn(eng):
                inst = None
                for (o, i_, K) in items:
                    inst = eng.transpose(o, i_, ident_f[0:K, 0:K])
                return inst
            return S.op("pe", fn, reads=list(reads) + ["ident_f"], writes=list(bank_toks))

        def act(out, in_, func, reads, writes, bias=None, scale=None):
            kw = {}
            if bias is not None:
                kw["bias"] = bias
            if scale is not None:
                kw["scale"] = scale
            return S.op("act", lambda e: e.activation(out=out, in_=in_, func=func, **kw), reads=reads, writes=writes)

        def tt(e, out, in0, in1, op, reads, writes):
            return S.op(e, lambda g: g.tensor_tensor(out=out, in0=in0, in1=in1, op=op), reads=reads, writes=writes)

        def stt(out, in0, scalar, in1, op0, op1, reads, writes):
            return S.op("dve", lambda g: g.scalar_tensor_tensor(out=out, in0=in0, scalar=scalar, in1=in1, op0=op0, op1=op1),
                        reads=reads, writes=writes)

        def ts(e, out, in0, s1, s2, op0, op1, reads, writes):
            if op1 is None:
                return S.op(e, lambda g: g.tensor_scalar(out=out, in0=in0, scalar1=s1, scalar2=None, op0=op0),
                            reads=reads, writes=writes)
            return S.op(e, lambda g: g.tensor_scalar(out=out, in0=in0, scalar1=s1, scalar2=s2, op0=op0, op1=op1),
                        reads=reads, writes=writes)

        def copy(e, out, in_, reads, writes):
            if e == "act":
                return S.op("act", lambda g: g.copy(out=out, in_=in_), reads=reads, writes=writes)
            return S.op(e, lambda g: g.tensor_copy(out=out, in_=in_), reads=reads, writes=writes)

        def rstd_from(sum_ap, sum_reads, sum_writes, scale, Tn, dst=None):
            r1, r1t = ST.next()
            act(r1[:, 0:Tn], sum_ap, AF.Sqrt, reads=list(sum_reads) + ["epsc"], writes=[r1t] + list(sum_writes),
                bias=epsc[:, 0:1], scale=scale)
            r2, r2t = ST.next() if dst is None else dst
            S.op("dve", lambda g: g.reciprocal(out=r2[:, 0:Tn], in_=r1[:, 0:Tn]), reads=[r1t], writes=[r2t])
            return r2, r2t

        Aflat = A[:].rearrange("p a b -> p (a b)")

        S.op("pool", lambda g: g.memset(ones_f[:], 1.0), writes=["ones_f"])
        S.op("pool", lambda g: g.memset(ident_f[:], 0.0), writes=["ident_f"])
        S.op("pool", lambda g: g.affine_select(out=ident_f[:], in_=ones_f[:], pattern=[[1, 128]],
                                                compare_op=ALU.is_equal, fill=0.0, base=0, channel_multiplier=-1),
             reads=["ones_f"], writes=["ident_f"])
        S.op("pool", lambda g: g.tensor_copy(out=ident_b[:], in_=ident_f[:]), reads=["ident_f"], writes=["ident_b"])
        S.op("pool", lambda g: g.memset(ones_b[:], 1.0), writes=["ones_b"])
        S.op("pool", lambda g: g.memset(epsc[:], EPS), writes=["epsc"])
        S.op("pool", lambda g: g.memset(epsc2[:], 1.0), writes=["epsc2"])
        for i in range(2):
            S.op("pool", lambda g, i=i: g.memset(xab[i][:, 0:KC - 1], 0.0), writes=[("xab", i)])

        pump(NUNITS)


        VA = Aflat[:, 0:128]
        VB = Aflat[0:64, 128:256]
        vecsA = [norm_g, ple_norm_g, conv_b, ln_a_g]
        vecsB = [ln_a_b, b_ple_gate]
        for i, v in enumerate(vecsA):
            S.dma("sp", ("ld", "va", i), Aflat[32 * i:32 * i + 32, 0:128], v.rearrange("l (j c) -> (l j) c", c=128),
                  writes=[("VA", i)])
        for i, v in enumerate(vecsB):
            S.dma("sp", ("ld", "vb", i), Aflat[32 * i:32 * i + 32, 128:256], v.rearrange("l (j c) -> (l j) c", c=128),
                  writes=[("VA", 4 + i)])
        CW = Aflat[0:DEPTH * KC, 1024:2048]
        S.dma("sp", ("ld", "cw"), CW, conv_w.rearrange("l k c -> (l k) c"), writes=[("P", "A", 2), ("P", "A", 3)])
        with nc.allow_non_contiguous_dma(reason="tiny 32-element gather of w_spatial[:, :, 0, 0]"):
            S.dma("sp", ("ld", "wm00"), wm00[:], w_spatial.rearrange("l g t s -> (l g) (t s)")[:, 0].partition_broadcast(NS),
                  writes=["wm00"])

        bk, bkt = PS.next()
        transposes([(bk[:, 0:128], VA, 128), (bk[0:128, 128:192], VB, 64)],
                   reads=[("P", "A", 0)] + [("VA", i) for i in range(6)], bank_toks=[bkt])
        copy("dve", cols[:, 0:192], bk[:, 0:192], reads=[], writes=[bkt, "cols"])
        for half in range(2):
            bk, bkt = PS.next()
            transposes([(bk[:, jj * 124:(jj + 1) * 124], CW[:, (half * 4 + jj) * 128:(half * 4 + jj + 1) * 128], DEPTH * KC)
                        for jj in range(4)], reads=[("P", "A", 2), ("P", "A", 3)], bank_toks=[bkt])
            copy("dve", cwT[:, half * 4:half * 4 + 4, :], bk[:, 0:496].rearrange("p (j k) -> p j k", j=4),
                 reads=[], writes=[bkt, "cwT"])
        for l in range(DEPTH):
            WS = Aflat[:, 2048 + (l % 2) * 1024: 3072 + (l % 2) * 1024].rearrange("p (g s) -> p g s", g=8)
            wst = ("P", "A", 4 + 2 * (l % 2))
            wst2 = ("P", "A", 5 + 2 * (l % 2))
            S.dma("sp", ("ld", "ws", l % 2), WS, w_spatial[l].rearrange("g t s -> t g s"), writes=[wst, wst2])
            S.op("pool", lambda g, WS=WS: g.affine_select(out=WS, in_=WS, pattern=[[0, 8], [-1, 128]],
                                                          compare_op=ALU.is_ge, fill=0.0, base=0, channel_multiplier=1),
                 reads=[wst, wst2], writes=[wst, wst2])
            for half in range(2):
                bk, bkt = PS.next()
                transposes([(bk[:, jj * 128:(jj + 1) * 128], WS[:, half * 4 + jj, :], 128) for jj in range(4)],
                           reads=[wst, wst2], bank_toks=[bkt])
                copy("dve", wmT[:, l, half * 4:half * 4 + 4, :], bk[:].rearrange("p (g t) -> p g t", g=4),
                     reads=[], writes=[bkt, "wmT"])
        tt("dve", wm00d[:], ident_b[0:NS, 0:NS].unsqueeze(1).to_broadcast([NS, DEPTH * 8, NS]),
           wm00[:].unsqueeze(2).to_broadcast([NS, DEPTH * 8, NS]), ALU.mult, reads=["ident_b", "wm00"], writes=["wm00d"])

        class Stream:
            pass

        P = Stream()
        P.pfx, P.is_s, P.Tn, P.NB, P.nt = "P", False, TT, 4, 128
        P.xT, P.hb, P.A, P.pain, P.pbin, P.vnb = xT, hb, A, pain, pbin, vnb
        P.xaf, P.pin, P.pT, P.small = xaf, pin, pT, small

        SX = Stream()
        SX.pfx, SX.is_s, SX.Tn, SX.NB, SX.nt = "S", True, NS, 1, NS
        SX.xT = sb("xT_s", [128, 8, NS], F32)
        SX.hb = sb("hb_s", [128, 8, NS], BF16)
        SX.A = sb("A_s", [128, 8, NS], F32)
        SX.pain = sb("pain_s", [128, 8, NS], BF16)
        SX.pbin = sb("pbin_s", [128, 8, TT], BF16)
        SX.vnb = sb("vnb_s", [NS, 1, D], BF16)
        SX.xaf = sb("xaf_s", [128, 8, NS], F32)
        SX.pin = sb("pin_s", [NS, 1, PLE], F32)
        SX.pT = sb("pT_s", [128, 2, NS], BF16)
        SX.small = sb("small_s", [NS, 64], F32)
        SX.xin = ystage[0]
        SX.yab = [sb(f"yab_s{i}", [128, NS], BF16) for i in range(2)]
        SX.ysq = [sb(f"ysq_s{i}", [128, NS], BF16) for i in range(2)]
        SX.rAb = sb("rAb_s", [128, NS], F32)
        SX.nmrb = sb("nmrb_s", [128, NS], F32)
        P.yab, P.ysq, P.rAb, P.nmrb = yab, ysq, rAb, nmrb
        P.pz, SX.pz = {}, {}

        def pump_through(l1):
            while pending_cv and pending_cv[0][0] <= l1:
                emit_cv(*pending_cv.pop(0))

        def T(st, name, i=None):
            return (st.pfx, name, i)

        def xxs_j(j):
            return SX.pbin[:, j, NS:NS + KC * NS].rearrange("p (k s) -> p k s", s=NS)

        def build_diag(l_, j_):
            jb_ = j_ % 2
            S.op("pool", lambda g: g.tensor_tensor(
                out=diag[jb_][:], in0=ident_b[:].unsqueeze(1).to_broadcast([128, KC, 128]),
                in1=cwT[:, j_, l_ * KC:(l_ + 1) * KC].unsqueeze(2).to_broadcast([128, KC, 128]), op=ALU.mult),
                reads=["ident_b", "cwT"], writes=[("diag", jb_)])

        build_diag(0, 0)
        build_diag(0, 1)

        def preload(func):
            act(epsc2[:, 1:2], epsc2[:, 0:1], func, reads=[], writes=["epsc2"])

        def HBK(st):
            return [[T(st, "hb", k)] for k in range(8)]

        ystage_i = [0]
        tl_list = [(t, l) for t in range(4) for l in range(DEPTH)]

        def load_x(t):
            S.dma("sp", ("ld", "x"), Aflat.rearrange("p (b d) -> p b d", b=4),
                  xp[t * TT:(t + 1) * TT, :].rearrange("(b p) d -> p b d", p=128),
                  writes=[("P", "A", j) for j in range(8)])

        def tile_start(st, t):
            Tn, NB, nt = st.Tn, st.NB, st.nt
            if st.is_s:
                S.dma("sp", ("ld", "xs"), st.xin[0:NS, :], xs[:, :], writes=[("ystage", 0)])
                rd = [("ystage", 0)]
            else:
                if t == 0:
                    load_x(0)
                rd = [T(st, "A", j) for j in range(8)]
            for k in range(8):
                bk, bkt = PS.next()
                items = []
                for b in range(NB):
                    if st.is_s:
                        src = st.xin[0:nt, k * 128:(k + 1) * 128]
                    else:
                        src = Aflat[0:nt, b * D + k * 128: b * D + (k + 1) * 128]
                    items.append((bk[:, b * 128: b * 128 + nt], src, nt))
                transposes(items, reads=rd, bank_toks=[bkt])
                copy("act" if k % 2 else "dve", st.xT[:, k, 0:Tn], bk[:, 0:Tn], reads=[], writes=[bkt, T(st, "xT", k)])

        def layer_loads(st, t, l):
            if st.is_s:
                S.dma("sp", ("ld", "ps"), st.pin[0:NS, 0, :], psm[l], writes=[T(st, "pin")])
            else:
                S.dma("sp", ("ld", "p"), st.pin[:], pp[l, t * TT:(t + 1) * TT, :].rearrange("(b p) d -> p b d", p=128),
                      writes=[T(st, "pin")])

        SSTG = [(ystage[0], ("ystage", 0)), (ystage[1], ("ystage", 1))]

        def sample_state_dma(l, pieces):
            for g4 in pieces:
                buf, btok = SSTG[g4 % 2]
                S.dma("sp", ("ld", "sc", g4 % 2), buf[0:120, :], sc[l, g4 * 4:(g4 + 1) * 4].rearrange("s k d -> (s k) d"),
                      writes=[btok])

        def sample_state_tr(pieces):
            st = SX
            for g4 in pieces:
                buf, btok = SSTG[g4 % 2]
                for half in range(2):
                    bk, bkt = PS.next()
                    transposes([(bk[:, jj * 120:(jj + 1) * 120], buf[0:120, (half * 4 + jj) * 128:(half * 4 + jj + 1) * 128], 120)
                                for jj in range(4)], reads=[btok], bank_toks=[bkt])
                    outv = st.pbin[:, half * 4:half * 4 + 4, NS:NS + KC * NS].rearrange("p j (k s) -> p j k s", s=NS)
                    copy("act" if half else "dve",
                         outv[:, :, 0:KC - 1, g4 * 4:(g4 + 1) * 4],
                         bk[:, 0:480].rearrange("p (j s k) -> p j k s", j=4, s=4),
                         reads=[], writes=[bkt] + [T(st, "pbin", half * 4 + jj) for jj in range(4)])

        def stage0(st, l, gvec, first):
            Tn = st.Tn
            for k in range(8):
                act(st.pbin[:, k, 0:Tn], st.xT[:, k, 0:Tn], AF.Square, reads=[T(st, "xT", k)], writes=[T(st, "pbin", k)])
            bk, bkt = PS.next()
            mm_group(bk[:, 0:Tn], [(ones_b[:], st.pbin[:, k, 0:Tn]) for k in range(8)],
                     reads=["ones_b"], bank_tok=bkt, per_reads=[[T(st, "pbin", k)] for k in range(8)])
            rs, rst = rstd_from(bk[:, 0:Tn], [], [bkt], 1.0 / D, Tn)
            for k in range(8):
                stt(st.hb[:, k, 0:Tn], st.xT[:, k, 0:Tn], col(gvec, l, k), rs[:, 0:Tn], ALU.mult, ALU.mult,
                    reads=[T(st, "xT", k), rst, "cols"], writes=[T(st, "hb", k)])

        def p_transposes(st):
            Tn, NB, nt = st.Tn, st.NB, st.nt
            for c2 in range(2):
                bk, bkt = PS.next()
                transposes([(bk[:, b * 128: b * 128 + nt], st.pin[0:nt, b, c2 * 128:(c2 + 1) * 128], nt) for b in range(NB)],
                           reads=[T(st, "pin")], bank_toks=[bkt])
                copy("act", st.pT[:, c2, 0:Tn], bk[:, 0:Tn], reads=[], writes=[bkt, T(st, "pT", c2)])

        def proj_group(st, w, wt, jj, hold=False):
            Tn = st.Tn
            pb, pbt = PS.next(hold=hold)
            mm_group(pb[:, 0:Tn], [(w[:, k * 512 + jj * 128: k * 512 + (jj + 1) * 128], st.hb[:, k, 0:Tn]) for k in range(8)],
                     reads=[wt], bank_tok=pbt, per_reads=HBK(st))
            return pb, pbt

        def s1_proj(st, tl, t, l, j):
            Tn = st.Tn
            jj = j % 4
            wal, walt = unit(tl, 0 if j < 4 else 2)
            wag, wagt = unit(tl, 1 if j < 4 else 3)
            pa, pat = proj_group(st, wal, walt, jj)
            pg, pgt = proj_group(st, wag, wagt, jj)
            sg, sgt = TMP.next()
            act(sg[:, 0:Tn], pg[:, 0:Tn], AF.Sigmoid, reads=[], writes=[pgt, sgt])
            jb = j % 2
            if st.is_s:
                tt("dve", xxs_j(j)[:, KC - 1, :], pa[:, 0:Tn], sg[:, 0:Tn], ALU.mult, reads=[sgt], writes=[pat, T(st, "pbin", j)])
            else:
                if t > 0:
                    copy("pool", xab[jb][:, 0:KC - 1], hist[:, l, j, :], reads=[("hist", l, j)], writes=[("xab", jb)])
                tt("dve", xab[jb][:, KC - 1:KC - 1 + TT], pa[:, 0:TT], sg[:, 0:TT], ALU.mult,
                   reads=[sgt], writes=[pat, ("xab", jb)])
                if t < 3:
                    copy("pool", hist[:, l, j, :], xab[jb][:, TT:TT + KC - 1], reads=[("xab", jb)], writes=[("hist", l, j)])
            if st.is_s or t == 3:
                nrow = NS if st.is_s else KC - 1
                c0 = 0 if st.is_s else TT - (KC - 1)
                tt("dve", st.xaf[:, j, 0:nrow], pa[:, c0:c0 + nrow], sg[:, c0:c0 + nrow], ALU.mult,
                   reads=[sgt], writes=[pat, T(st, "xaf", j)])

        def s1_conv(st, l, j):
            Tn = st.Tn
            jb = j % 2
            py, pyt = PS.next()
            if st.is_s:
                pairs = [(diag[jb][:, k, :], xxs_j(j)[:, k, :]) for k in range(KC)]
                rd = [("diag", jb), T(st, "pbin", j)]
            else:
                pairs = [(diag[jb][:, k, :], xab[jb][:, k:k + TT]) for k in range(KC)]
                rd = [("diag", jb), ("xab", jb)]
            mm_group(py[:, 0:Tn], pairs, reads=rd, bank_tok=pyt)
            yq, yqt = st.ysq[jb], T(st, "ysq", jb)
            ya, yat = st.yab[jb], T(st, "yab", jb)
            act(st.A[:, j, 0:Tn], py[:, 0:Tn], AF.Identity, reads=["cols"], writes=[pyt, T(st, "A", j)], bias=col(V_CONVB, l, j))
            act(yq[:, 0:Tn], py[:, 0:Tn], AF.Square, reads=["cols"], writes=[pyt, yqt], bias=col(V_CONVB, l, j))
            copy("pool", ya[:, 0:Tn], st.A[:, j, 0:Tn], reads=[T(st, "A", j)], writes=[yat])

        def s1_stats(st, j):
            Tn = st.Tn
            jb = j % 2

            def fn(eng):
                if st.is_s:
                    eng.matmul(st.ssum, lhsT=ones_b[:], rhs=st.yab[jb][:, 0:Tn], start=(j == 0), stop=(j == 7))
                    return eng.matmul(st.ssq, lhsT=ones_b[:], rhs=st.ysq[jb][:, 0:Tn], start=False, stop=(j == 7),
                                      skip_group_check=True)
                eng.matmul(st.ssum, lhsT=ones_b[:], rhs=st.yab[jb][:, 0:Tn], start=(j == 0), stop=(j == 7))
                return eng.matmul(st.ssq, lhsT=ones_b[:], rhs=st.ysq[jb][:, 0:Tn], start=(j == 0), stop=(j == 7))
            S.op("pe", fn, reads=["ones_b", T(st, "yab", jb), T(st, "ysq", jb)], writes=list(st.stat_toks))

        def state_out(st, l):
            nrow = NS if st.is_s else KC - 1
            bk0, bk0t = PS.next()
            bk1, bk1t = PS.next()
            transposes([((bk0 if j < 4 else bk1)[0:nrow, (j % 4) * 128:(j % 4 + 1) * 128], st.xaf[:, j, 0:nrow], 128)
                        for j in range(8)], reads=[T(st, "xaf", j) for j in range(8)], bank_toks=[bk0t, bk1t])
            cstage, CST_T = ystage[1], ("ystage", 1)
            copy("act", cstage[0:nrow, 0:512], bk0[0:nrow, :], reads=[], writes=[bk0t, CST_T])
            copy("act", cstage[0:nrow, 512:1024], bk1[0:nrow, :], reads=[], writes=[bk1t, CST_T])
            if st.is_s:
                S.dma("pool", ("o", "y", 1), cs[l, :, KC - 2, :], cstage[0:nrow, :], reads=[CST_T], is_output=True)
            else:
                S.dma("pool", ("o", "y", 1), cp[l], cstage[0:nrow, :], reads=[CST_T], is_output=True)

        def stage3(st, tl, t, l):
            NB, nt = st.NB, st.nt
            wv0, wv0t = unit(tl, 4)
            wv1, wv1t = unit(tl, 5)
            sm, smt = st.small, T(st, "small")
            for b in range(NB):
                pv = []
                for half, (wv, wvt) in enumerate(((wv0, wv0t), (wv1, wv1t))):
                    bk, bkt = PS.next()
                    mm_group(bk[0:nt, :], [(st.hb[:, k, b * 128: b * 128 + nt], wv[:, k * 512:(k + 1) * 512]) for k in range(8)],
                             reads=[wvt], bank_tok=bkt, per_reads=HBK(st))
                    pv.append((bk, bkt))
                S.op("dve", lambda g, pv=pv: g.bn_stats(out=sm[0:nt, 0:6], in_=pv[0][0][0:nt, :]), reads=[], writes=[pv[0][1], smt])
                S.op("dve", lambda g, pv=pv: g.bn_stats(out=sm[0:nt, 6:12], in_=pv[1][0][0:nt, :]), reads=[], writes=[pv[1][1], smt])
                S.op("dve", lambda g: g.bn_aggr(out=sm[0:nt, 12:14], in_=sm[0:nt, 0:12]), reads=[smt], writes=[smt])
                act(sm[0:nt, 14:15], sm[0:nt, 13:14], AF.Sqrt, reads=[smt, "epsc"], writes=[smt], bias=epsc[0:nt, 0:1], scale=1.0)
                S.op("dve", lambda g: g.reciprocal(out=sm[0:nt, 15:16], in_=sm[0:nt, 14:15]), reads=[smt], writes=[smt])
                stt(sm[0:nt, 16:17], sm[0:nt, 12:13], -1.0, sm[0:nt, 15:16], ALU.mult, ALU.mult, reads=[smt], writes=[smt])
                for half in range(2):
                    act(tmpv[0:nt, half * 512:(half + 1) * 512], pv[half][0][0:nt, :], AF.Identity,
                        reads=[smt], writes=[pv[half][1], "tmpv"], bias=sm[0:nt, 16:17], scale=sm[0:nt, 15:16])
                tt("pool", tmpv[0:nt, :], tmpv[0:nt, :], lnvg[0:nt, :], ALU.mult, reads=["lnvg"], writes=["tmpv"])
                tt("pool", st.vnb[0:nt, b, :], tmpv[0:nt, :], lnvb[0:nt, :], ALU.add, reads=["tmpv", "lnvb"], writes=[T(st, "vnb", b)])
                if st.is_s or (t == 3 and b == NB - 1):
                    vnf, VNF_T = ystage[0], ("ystage", 0)
                    tt("pool", vnf[0:nt, :], tmpv[0:nt, :], lnvb[0:nt, :], ALU.add, reads=["tmpv", "lnvb"], writes=[VNF_T])
                    S.dma("pool", ("o", "y", 0), vs[l] if st.is_s else vp[l], vnf[0:nt, :], reads=[VNF_T], is_output=True)

        def stage4(st):
            Tn = st.Tn
            mean, meant = ST.next()
            ts("dve", mean[:, 0:Tn], st.ssum, 1.0 / D, None, ALU.mult, None, reads=[], writes=list(st.stat_toks) + [meant])
            msq, msqt = ST.next()
            tt("dve", msq[:, 0:Tn], mean[:, 0:Tn], mean[:, 0:Tn], ALU.mult, reads=[meant], writes=[msqt])
            var, vart = ST.next()
            stt(var[:, 0:Tn], st.ssq, 1.0 / D, msq[:, 0:Tn], ALU.mult, ALU.subtract, reads=[msqt], writes=list(st.stat_toks) + [vart])
            for tok in st.stat_toks:
                PS.release(tok)
            rA, rAt = rstd_from(var[:, 0:Tn], [vart], [], 1.0, Tn, dst=(st.rAb, T(st, "rAb")))
            nmr, nmrt = st.nmrb, T(st, "nmrb")
            stt(nmr[:, 0:Tn], mean[:, 0:Tn], -1.0, rA[:, 0:Tn], ALU.mult, ALU.mult, reads=[meant, rAt], writes=[nmrt])
            st.rA, st.rAt, st.nmr, st.nmrt = rA, rAt, nmr, nmrt

        def s5_pe(st, tl, j, hold=False):
            jj = j % 4
            wz, wzt = unit(tl, 6 if j < 4 else 7)
            st.pz[j] = proj_group(st, wz, wzt, jj, hold=hold)

        def s5_chunk(st, tl, l, j):
            Tn = st.Tn
            if j not in st.pz:
                s5_pe(st, tl, j)
            pz, pzt = st.pz.pop(j)
            PS.release(pzt)
            sz, szt = TMP.next()
            act(sz[:, 0:Tn], pz[:, 0:Tn], AF.Silu, reads=[], writes=[pzt, szt])
            t1, t1t = TMP.next()
            tt("pool", t1[:, 0:Tn], st.A[:, j, 0:Tn], st.rA[:, 0:Tn], ALU.mult, reads=[T(st, "A", j), st.rAt], writes=[t1t])
            tt("dve", t1[:, 0:Tn], t1[:, 0:Tn], st.nmr[:, 0:Tn], ALU.add, reads=[t1t, st.nmrt], writes=[t1t])
            act(t1[:, 0:Tn], t1[:, 0:Tn], AF.Silu, reads=[t1t, "cols"], writes=[t1t],
                bias=col(V_LNAB, l, j), scale=col(V_LNAG, l, j))
            tt("dve", st.pain[:, j, 0:Tn], t1[:, 0:Tn], sz[:, 0:Tn], ALU.mult, reads=[t1t, szt], writes=[T(st, "pain", j)])

        def s6_group(st, tl, l, g):
            Tn, NB, nt = st.Tn, st.NB, st.nt
            gg = g % 4
            wu, wut = unit(tl, 8 if g < 4 else 10)
            wzb, wzbt = unit(tl, 9 if g < 4 else 11)
            psx, psxt = PS.next()

            def fn(eng):
                inst = None
                for b in range(NB):
                    o = psx[:, b * 128: b * 128 + nt]
                    if st.is_s:
                        rhs = wm00d[0:NS, l * 8 + g, :]
                        brow = bsps[0:1, g, :]
                    else:
                        rhs = wmT[:, l, g, :]
                        brow = bsp[0:1, g * 128:(g + 1) * 128]
                    eng.matmul(o, lhsT=st.vnb[0:nt, b, g * 128:(g + 1) * 128], rhs=rhs, start=True, stop=False)
                    inst = eng.matmul(o, lhsT=ones_b[0:1, :], rhs=brow, start=False, stop=True)
                return inst
            S.op("pe", fn, reads=[T(st, "vnb", b) for b in range(NB)] + ["wmT", "bsp", "bsps", "wm00d", "ones_b"], writes=[psxt])
            pu, put = proj_group(st, wu, wut, gg)
            pzb, pzbt = proj_group(st, wzb, wzbt, gg)
            szb, szbt = TMP.next()
            act(szb[:, 0:Tn], pzb[:, 0:Tn], AF.Silu, reads=[], writes=[pzbt, szbt])
            t1, t1t = TMP.next()
            tt("dve", t1[:, 0:Tn], pu[:, 0:Tn], szb[:, 0:Tn], ALU.mult, reads=[szbt], writes=[put, t1t])
            tt("dve", st.pbin[:, g, 0:Tn], psx[:, 0:Tn], t1[:, 0:Tn], ALU.mult, reads=[t1t], writes=[psxt, T(st, "pbin", g)])

        def in_group(st, w, wt, jj, src, srcname):
            Tn = st.Tn
            pb, pbt = PS.next()
            mm_group(pb[:, 0:Tn], [(w[:, k * 512 + jj * 128: k * 512 + (jj + 1) * 128], src[:, k, 0:Tn]) for k in range(8)],
                     reads=[wt], bank_tok=pbt, per_reads=[[T(st, srcname, k)] for k in range(8)])
            return pb, pbt

        def s7_chunk(st, tl, j):
            Tn = st.Tn
            jj = j % 4
            wpa, wpat = unit(tl, 12 if j < 4 else 14)
            wga, wgat = unit(tl, 13 if j < 4 else 15)
            pya, pyat = in_group(st, wpa, wpat, jj, st.pain, "pain")
            pga, pgat = proj_group(st, wga, wgat, jj)
            sga, sgat = TMP.next()
            act(sga[:, 0:Tn], pga[:, 0:Tn], AF.Sigmoid, reads=[], writes=[pgat, sgat])
            tt("dve", st.A[:, j, 0:Tn], pya[:, 0:Tn], sga[:, 0:Tn], ALU.mult, reads=[sgat], writes=[pyat, T(st, "A", j)])

        def s8_chunk(st, tl, j):
            Tn = st.Tn
            jj = j % 4
            wpb, wpbt = unit(tl, 16 if j < 4 else 18)
            wgb, wgbt = unit(tl, 17 if j < 4 else 19)
            pyb, pybt = in_group(st, wpb, wpbt, jj, st.pbin, "pbin")
            pgb, pgbt = proj_group(st, wgb, wgbt, jj)
            sgb, sgbt = TMP.next()
            act(sgb[:, 0:Tn], pgb[:, 0:Tn], AF.Sigmoid, reads=[], writes=[pgbt, sgbt])
            t1, t1t = TMP.next()
            tt("dve", t1[:, 0:Tn], pyb[:, 0:Tn], sgb[:, 0:Tn], ALU.mult, reads=[sgbt], writes=[pybt, t1t])
            tt("dve" if j >= 6 else "pool", st.pain[:, j, 0:Tn], st.A[:, j, 0:Tn], t1[:, 0:Tn], ALU.add,
               reads=[T(st, "A", j), t1t], writes=[T(st, "pain", j)])

        def s9_chunk(st, tl, j):
            Tn = st.Tn
            jj = j % 4
            wo, wot = unit(tl, 20 if j < 4 else 21)
            po, pot = in_group(st, wo, wot, jj, st.pain, "pain")
            tt("dve", st.xT[:, j, 0:Tn], st.xT[:, j, 0:Tn], po[:, 0:Tn], ALU.add, reads=[T(st, "xT", j)], writes=[pot, T(st, "xT", j)])

        def s10_pw(st, tl):
            Tn = st.Tn
            wpl, wplt = unit(tl, 24)
            for j in range(8):
                ppw, ppwt = PS.next()
                mm_group(ppw[:, 0:Tn], [(wpl[:, c2 * D + j * 128: c2 * D + (j + 1) * 128], st.pT[:, c2, 0:Tn]) for c2 in range(2)],
                         reads=[T(st, "pT", 0), T(st, "pT", 1), wplt], bank_tok=ppwt)
                copy("act", st.A[:, j, 0:Tn], ppw[:, 0:Tn], reads=[], writes=[ppwt, T(st, "A", j)])

        def s10_chunk(st, tl, l, j):
            Tn = st.Tn
            jj = j % 4
            wpg, wpgt = unit(tl, 22 if j < 4 else 23)
            pgt_, pgtt = proj_group(st, wpg, wpgt, jj)
            gp, gpt = TMP.next()
            act(gp[:, 0:Tn], pgt_[:, 0:Tn], AF.Sigmoid, reads=["cols"], writes=[pgtt, gpt], bias=col(V_BPG, l, j))
            t1, t1t = TMP.next()
            tt("dve", t1[:, 0:Tn], st.A[:, j, 0:Tn], gp[:, 0:Tn], ALU.mult, reads=[gpt, T(st, "A", j)], writes=[t1t])
            tt("dve" if j >= 6 else "pool", st.xT[:, j, 0:Tn], st.xT[:, j, 0:Tn], t1[:, 0:Tn], ALU.add,
               reads=[T(st, "xT", j), t1t], writes=[T(st, "xT", j)])

        def final_out(st, t):
            NB, nt = st.NB, st.nt
            sm, smt = st.small, T(st, "small")
            for b in range(NB):
                bk0, bk0t = PS.next()
                bk1, bk1t = PS.next()
                transposes([((bk0 if k < 4 else bk1)[0:nt, (k % 4) * 128:(k % 4 + 1) * 128], st.xT[:, k, b * 128: b * 128 + nt], 128)
                            for k in range(8)], reads=[T(st, "xT", k) for k in range(8)], bank_toks=[bk0t, bk1t])
                S.op("dve", lambda g: g.bn_stats(out=sm[0:nt, 0:6], in_=bk0[0:nt, :]), reads=[], writes=[bk0t, smt])
                S.op("dve", lambda g: g.bn_stats(out=sm[0:nt, 6:12], in_=bk1[0:nt, :]), reads=[], writes=[bk1t, smt])
                S.op("dve", lambda g: g.bn_aggr(out=sm[0:nt, 12:14], in_=sm[0:nt, 0:12]), reads=[smt], writes=[smt])
                stt(sm[0:nt, 14:15], sm[0:nt, 12:13], sm[0:nt, 12:13], sm[0:nt, 13:14], ALU.mult, ALU.add,
                    reads=[smt], writes=[smt])
                act(sm[0:nt, 15:16], sm[0:nt, 14:15], AF.Sqrt, reads=[smt, "epsc"], writes=[smt], bias=epsc[0:nt, 0:1], scale=1.0)
                S.op("dve", lambda g: g.reciprocal(out=sm[0:nt, 16:17], in_=sm[0:nt, 15:16]), reads=[smt], writes=[smt])
                yi = ystage_i[0] % 2
                ystage_i[0] += 1
                yst = ystage[yi]
                for half, (bkx, bkxt) in enumerate(((bk0, bk0t), (bk1, bk1t))):
                    stt(yst[0:nt, half * 512:(half + 1) * 512], bkx[0:nt, :], sm[0:nt, 16:17], lnvg[0:nt, half * 512:(half + 1) * 512],
                        ALU.mult, ALU.mult, reads=[smt, "lnvg"], writes=[bkxt, ("ystage", yi)])
                if st.is_s:
                    S.dma("pool", ("o", "y", yi), ys[:, :], yst[0:nt, :], reads=[("ystage", yi)], is_output=True)
                else:
                    r0 = t * TT + b * 128
                    S.dma("pool", ("o", "y", yi), yp[r0:r0 + 128, :], yst[0:nt, :], reads=[("ystage", yi)], is_output=True)

        RIDER_TILE = 1
        for tl, (t, l) in enumerate(tl_list):
            streams = [P] + ([SX] if t == RIDER_TILE else [])
            nxt = tl_list[tl + 1] if tl + 1 < len(tl_list) else None
            pre_state = nxt is not None and nxt[0] == RIDER_TILE
            cvp = (t == 0 and l < DEPTH - 1)

            if l == 0:
                for st in streams:
                    tile_start(st, t)
            S.dma("sp", ("ld", "lnvg"), lnvg[:], ln_v_g[l].partition_broadcast(128), writes=["lnvg"])
            S.dma("sp", ("ld", "lnvb"), lnvb[:], ln_v_b[l].partition_broadcast(128), writes=["lnvb"])
            S.dma("pool", ("ld", "bsp"), bsp[:], b_spatial[l].rearrange("g t -> (g t)").unsqueeze(0), writes=["bsp"])
            if SX in streams:
                copy("dve", bsps[:], bsp[:].rearrange("o (g t) -> o g t", t=128)[:, :, 0:1].to_broadcast([1, 8, NS]),
                     reads=["bsp"], writes=["bsps"])
            for st in streams:
                layer_loads(st, t, l)
            emit_stream()
            if tl == 8:
                S.dma("sp", ("o", "csrows"), cs[:, :, 0:KC - 2, :], sc[:, :, 1:KC - 1, :], is_output=True)

            preload(AF.Sqrt)
            for st in streams:
                stage0(st, l, V_NORMG, True)
            preload(AF.Sigmoid)
            for st in streams:
                p_transposes(st)

            for st in streams:
                if st.is_s:
                    bk, bkt = PS.next(hold=True)
                    st.ssum, st.ssq, st.stat_toks = bk[:, 0:NS], bk[:, NS:2 * NS], [bkt]
                else:
                    b1, b1t = PS.next(hold=True)
                    b2, b2t = PS.next(hold=True)
                    st.ssum, st.ssq, st.stat_toks = b1[:, 0:TT], b2[:, 0:TT], [b1t, b2t]
            for step in range(10):
                if step < 8:
                    for st in streams:
                        s1_proj(st, tl, t, l, step)
                    if step == 3:
                        done(tl, 0, 1)
                    if step == 7:
                        done(tl, 2, 3)
                        preload(AF.Sqrt)
                    if cvp:
                        pump(1)
                if 1 <= step <= 8:
                    for st in streams:
                        s1_conv(st, l, step - 1)
                if 1 <= step <= 6:
                    build_diag(l, step + 1)
                if step == 9:
                    s5_pe(P, tl, 0, hold=True)
                    s5_pe(P, tl, 1, hold=True)
                if 2 <= step <= 9:
                    for st in streams:
                        s1_stats(st, step - 2)
            for st in streams:
                if st.is_s or t == 3:
                    state_out(st, l)

            for st in streams:
                stage4(st)
            preload(AF.Silu)
            for j in range(8):
                for st in streams:
                    s5_chunk(st, tl, l, j)
                if cvp:
                    pump(1)
                if j == 3:
                    done(tl, 6)
                if j == 7:
                    done(tl, 7)
            preload(AF.Sqrt)
            for st in streams:
                stage3(st, tl, t, l)
                if cvp:
                    pump(2)
            done(tl, 4, 5)
            if l == DEPTH - 1:
                S.dma("sp", ("ld", "lnvg"), lnvg[:], final_g.partition_broadcast(128), writes=["lnvg"])
            if pre_state:
                sample_state_dma(nxt[1], (0, 1))

            preload(AF.Sigmoid)
            if tl + 1 < len(tl_list):
                build_diag(tl_list[tl + 1][1], 0)
                build_diag(tl_list[tl + 1][1], 1)
            for j in range(8):
                for st in streams:
                    s7_chunk(st, tl, j)
                if j == 3:
                    done(tl, 12, 13)
                if j == 7:
                    done(tl, 14, 15)
            if pre_state:
                sample_state_tr((0, 1))
                sample_state_dma(nxt[1], (2, 3))
            preload(AF.Silu)
            for g in range(8):
                for st in streams:
                    s6_group(st, tl, l, g)
                if cvp:
                    pump(1)
                if g == 3:
                    done(tl, 8, 9)
                if g == 7:
                    done(tl, 10, 11)
                    preload(AF.Sigmoid)
            if cvp:
                pump_through(l + 1)
            if pre_state:
                sample_state_tr((2, 3))
            for j in range(8):
                for st in streams:
                    s8_chunk(st, tl, j)
                if j == 3:
                    done(tl, 16, 17)
                if j == 7:
                    done(tl, 18, 19)
            for j in range(8):
                for st in streams:
                    s9_chunk(st, tl, j)
                if j == 3:
                    done(tl, 20)
                if j == 7:
                    done(tl, 21)
            preload(AF.Sqrt)
            for st in streams:
                stage0(st, l, V_PLEG, False)
            for st in streams:
                s10_pw(st, tl)
            done(tl, 24)
            preload(AF.Sigmoid)
            for j in range(8):
                for st in streams:
                    s10_chunk(st, tl, l, j)
                if j == 3:
                    done(tl, 22)
                if j == 7:
                    done(tl, 23)
            if l == DEPTH - 1:
                if t + 1 < 4:
                    load_x(t + 1)
                for st in streams:
                    final_out(st, t)

        S.finish()
    return nc


_NC_CACHE = {}


def kernel(**inputs):
    inp = {k: np.ascontiguousarray(np.asarray(v)) for k, v in inputs.items()}
    if "nc" not in _NC_CACHE:
        _NC_CACHE["nc"] = build_program()
    nc = _NC_CACHE["nc"]
    wnames = ["norm_g", "w_in", "conv_w", "conv_b", "ln_a_g", "ln_a_b", "w_proj_a", "ln_v_g", "ln_v_b",
              "w_spatial", "b_spatial", "w_proj_b", "w_out", "ple_norm_g", "w_ple_gate", "b_ple_gate",
              "w_ple", "final_g"]
    in_maps = []
    for c in range(NCORES):
        m = {
            "xp": inp["x_prompt"][c],
            "xs": np.ascontiguousarray(inp["x_sample"][c * NS:(c + 1) * NS, 0, :]),
            "sc": np.ascontiguousarray(inp["state_conv"][:, c * NS:(c + 1) * NS]),
            "pp": np.ascontiguousarray(inp["p_prompt"][:, c]),
            "psm": np.ascontiguousarray(inp["p_sample"][:, c * NS:(c + 1) * NS, 0, :]),
        }
        for w in wnames:
            m[w] = inp[w]
        in_maps.append(m)
    res = run_bass_kernel_spmd(nc, in_maps, core_ids=list(range(NCORES)))
    R = res.results
    y_prompt = np.stack([R[c]["yp"] for c in range(NCORES)], axis=0)
    y_sample = np.concatenate([R[c]["ys"] for c in range(NCORES)], axis=0)[:, None, :]
    conv_prompt = np.stack([R[c]["cp"] for c in range(NCORES)], axis=1)
    conv_sample = np.concatenate([R[c]["cs"] for c in range(NCORES)], axis=1)
    vrows_prompt = np.stack([R[c]["vp"] for c in range(NCORES)], axis=1)
    vrows_sample = np.concatenate([R[c]["vs"] for c in range(NCORES)], axis=1)[:, :, None, :]
    return (y_prompt.astype(np.float32), y_sample.astype(np.float32), conv_prompt.astype(np.float32),
            conv_sample.astype(np.float32), vrows_prompt.astype(np.float32), vrows_sample.astype(np.float32))
```

```python
import numpy as np
from contextlib import ExitStack
import concourse.bass as bass
import concourse.mybir as mybir
from concourse.bass_utils import run_bass_kernel_spmd

F32 = mybir.dt.float32
BF16 = mybir.dt.bfloat16
AF = mybir.ActivationFunctionType
ALU = mybir.AluOpType

D = 1024
DEPTH = 4
SEQ = 2048
NS = 16
KC = 31
PLE = 256
EPS = 1e-6
TT = 512
NCORES = 8
NUNITS = 25
NSLOT = 4
UW = 4096

UNIT_SRC = [
    ("w_in", 0), ("w_in", 1024), ("w_in", 512), ("w_in", 1536),
    ("w_in", 4096), ("w_in", 4608),
    ("w_in", 2048), ("w_in", 2560),
    ("w_in", 3072), ("w_in", 5120), ("w_in", 3584), ("w_in", 5632),
    ("w_proj_a", 0), ("w_in", 6144), ("w_proj_a", 512), ("w_in", 6656),
    ("w_proj_b", 0), ("w_in", 7168), ("w_proj_b", 512), ("w_in", 7680),
    ("w_out", 0), ("w_out", 512),
    ("w_ple_gate", 0), ("w_ple_gate", 512),
    ("w_ple", 0),
]


ORDER = [0, 1, 2, 3, 6, 7, 4, 5, 12, 13, 14, 15, 8, 9, 10, 11, 16, 17, 18, 19, 20, 21, 24, 22, 23]
POS = {u: p for p, u in enumerate(ORDER)}


def cv_group(l, u):
    return u if l <= 1 else POS[u] // 5


class Sched:
    def __init__(self, nc, es):
        self.nc = nc
        self.es = es
        self.eng = {"pe": nc.tensor, "act": nc.scalar, "dve": nc.vector, "pool": nc.gpsimd, "sp": nc.sync}
        self.sem = {k: es.enter_context(nc.semaphore("sem_" + k)) for k in self.eng}
        self.cnt = {k: 0 for k in self.eng}
        self.waited = {k: {} for k in self.eng}
        self.last_w = {}
        self.readers = {}
        self.dsem = {}
        self.dcnt = {}
        self.out_events = []

    def _need(self, e, ev, need):
        if ev is None:
            return
        key, val, src = ev
        cur = need.get(key)
        if cur is None or cur[0] < val:
            need[key] = (val, src)

    def _waits(self, e, reads, writes):
        need = {}
        for r in reads:
            ev = self.last_w.get(r)
            if ev is not None:
                if not (ev[2] == e and e == "pe"):
                    self._need(e, ev, need)
        for w in writes:
            ev = self.last_w.get(w)
            if ev is not None and not (ev[2] == e and e == "pe"):
                self._need(e, ev, need)
            for ev in self.readers.get(w, ()):
                if not (ev[2] == e and e == "pe"):
                    self._need(e, ev, need)
        wd = self.waited[e]
        for key, (val, src) in need.items():
            if wd.get(key, 0) >= val:
                continue
            semh = self.sem[key] if key in self.sem else self.dsem[key]
            self.eng[e].wait_ge(semh, val)
            wd[key] = val

    def _record(self, ev, reads, writes):
        for w in writes:
            self.last_w[w] = ev
            self.readers[w] = []
        for r in reads:
            if r in writes:
                continue
            self.readers.setdefault(r, []).append(ev)

    def op(self, e, fn, reads=(), writes=()):
        self._waits(e, reads, writes)
        inst = fn(self.eng[e])
        self.cnt[e] += 1
        inst.then_inc(self.sem[e], 1)
        ev = (e, self.cnt[e], e)
        self._record(ev, reads, writes)
        return ev

    def dma(self, q, key, out, in_, reads=(), writes=(), is_output=False, **kw):
        self._waits(q, reads, writes)
        if key not in self.dsem:
            self.dsem[key] = self.es.enter_context(self.nc.semaphore("d_" + "_".join(str(x) for x in key)))
            self.dcnt[key] = 0
        inst = self.eng[q].dma_start(out=out, in_=in_, **kw)
        self.dcnt[key] += 16
        inst.then_inc(self.dsem[key], 16)
        ev = (key, self.dcnt[key], "dma")
        self._record(ev, reads, writes)
        if is_output:
            self.out_events.append(ev)
        return ev

    def finish(self):
        final = {}
        for key, val, _ in self.out_events:
            final[key] = max(final.get(key, 0), val)
        for key, val in final.items():
            self.nc.sync.wait_ge(self.dsem[key], val)


class Ring:
    def __init__(self, aps, name):
        self.aps = aps
        self.name = name
        self.i = 0
        self.held = set()

    def next(self, hold=False):
        n = len(self.aps)
        for _ in range(n):
            i = self.i
            self.i = (self.i + 1) % n
            if i not in self.held:
                if hold:
                    self.held.add(i)
                return self.aps[i], (self.name, i)
        raise RuntimeError("ring exhausted")

    def release(self, tok):
        self.held.discard(tok[1])


def build_program():
    nc = bass.Bass("TRN2", target_bir_lowering=False)

    def din(name, shape):
        return nc.dram_tensor(name, list(shape), F32, kind="ExternalInput").ap()

    def dout(name, shape):
        return nc.dram_tensor(name, list(shape), F32, kind="ExternalOutput").ap()

    xp = din("xp", [SEQ, D])
    xs = din("xs", [NS, D])
    sc = din("sc", [DEPTH, NS, KC - 1, D])
    pp = din("pp", [DEPTH, SEQ, PLE])
    psm = din("psm", [DEPTH, NS, PLE])
    norm_g = din("norm_g", [DEPTH, D])
    w_in = din("w_in", [DEPTH, D, 8 * D])
    conv_w = din("conv_w", [DEPTH, KC, D])
    conv_b = din("conv_b", [DEPTH, D])
    ln_a_g = din("ln_a_g", [DEPTH, D])
    ln_a_b = din("ln_a_b", [DEPTH, D])
    w_proj_a = din("w_proj_a", [DEPTH, D, D])
    ln_v_g = din("ln_v_g", [DEPTH, D])
    ln_v_b = din("ln_v_b", [DEPTH, D])
    w_spatial = din("w_spatial", [DEPTH, 8, 128, 128])
    b_spatial = din("b_spatial", [DEPTH, 8, 128])
    w_proj_b = din("w_proj_b", [DEPTH, D, D])
    w_out = din("w_out", [DEPTH, D, D])
    ple_norm_g = din("ple_norm_g", [DEPTH, D])
    w_ple_gate = din("w_ple_gate", [DEPTH, D, D])
    b_ple_gate = din("b_ple_gate", [DEPTH, D])
    w_ple = din("w_ple", [DEPTH, PLE, D])
    final_g = din("final_g", [D])
    wsrc = {"w_in": w_in, "w_proj_a": w_proj_a, "w_proj_b": w_proj_b, "w_out": w_out,
            "w_ple_gate": w_ple_gate, "w_ple": w_ple}

    yp = dout("yp", [SEQ, D])
    ys = dout("ys", [NS, D])
    cp = dout("cp", [DEPTH, KC - 1, D])
    cs = dout("cs", [DEPTH, NS, KC - 1, D])
    vp = dout("vp", [DEPTH, 128, D])
    vs = dout("vs", [DEPTH, NS, D])

    wscr = nc.dram_tensor("wscr", [DEPTH, NUNITS, 128, UW], BF16, kind="Internal").ap()

    with ExitStack() as es:
        S = Sched(nc, es)

        def sb(name, shape, dt):
            return es.enter_context(nc.sbuf_tensor(name, list(shape), dt))

        xT = sb("xT", [128, 8, TT], F32)
        hb = sb("hb", [128, 8, TT], BF16)
        A = sb("A", [128, 8, TT], F32)
        pain = sb("pain", [128, 8, TT], BF16)
        pbin = sb("pbin", [128, 8, TT], BF16)
        vnb = sb("vnb", [128, 4, D], BF16)
        tmpv = sb("tmpv", [128, D], F32)
        xab = [sb(f"xab{i}", [128, KC - 1 + TT], BF16) for i in range(2)]
        hist = sb("hist", [128, DEPTH, 8, KC - 1], BF16)
        xaf = sb("xaf", [128, 8, 32], F32)
        diag = [sb(f"diag{i}", [128, KC, 128], BF16) for i in range(2)]
        yab = [sb(f"yab{i}", [128, TT], BF16) for i in range(2)]
        ysq = [sb(f"ysq{i}", [128, TT], BF16) for i in range(2)]
        tmps = [sb(f"tmp{i}", [128, TT], F32) for i in range(6)]
        sts = [sb(f"st{i}", [128, TT], F32) for i in range(4)]
        rAb = sb("rAb", [128, TT], F32)
        nmrb = sb("nmrb", [128, TT], F32)
        pin = sb("pin", [128, 4, PLE], F32)
        pT = sb("pT", [128, 2, TT], BF16)
        ystage = [sb(f"ystage{i}", [128, D], F32) for i in range(2)]
        lnvg = sb("lnvg", [128, D], F32)
        lnvb = sb("lnvb", [128, D], F32)
        wmT = sb("wmT", [128, DEPTH, 8, 128], BF16)
        bsp = sb("bsp", [1, 8 * 128], BF16)
        bsps = sb("bsps", [1, 8, NS], BF16)
        wm00 = sb("wm00", [NS, DEPTH * 8], F32)
        wm00d = sb("wm00d", [NS, DEPTH * 8, NS], BF16)
        cols = sb("cols", [128, 192], F32)
        cwT = sb("cwT", [128, 8, DEPTH * KC], F32)
        ident_f = sb("ident_f", [128, 128], F32)
        ident_b = sb("ident_b", [128, 128], BF16)
        ones_f = sb("ones_f", [128, 128], F32)
        ones_b = sb("ones_b", [128, 128], BF16)
        epsc = sb("epsc", [128, 1], F32)
        epsc2 = sb("epsc2", [128, 2], F32)
        small = sb("small", [128, 64], F32)
        wslots = [sb(f"wslot{i}", [128, UW], BF16) for i in range(NSLOT)]
        psb = [es.enter_context(nc.psum_tensor(f"ps{i}", [128, 512], F32)) for i in range(8)]

        vnf = ystage[0]
        VNF_T = ("ystage", 0)
        cstage = ystage[1]
        CST_T = ("ystage", 1)

        def xxs_j(j):
            return pbin[:, j, NS:NS + KC * NS].rearrange("p (k s) -> p k s", s=NS)

        PS = Ring(psb, "ps")
        TMP = Ring(tmps, "tmp")
        ST = Ring(sts, "st")

        def col(vec, l, j):
            c = vec * 32 + l * 8 + j
            return cols[:, c:c + 1]
        V_NORMG, V_PLEG, V_CONVB, V_LNAG, V_LNAB, V_BPG = range(6)

        pending_cv = [(l, u) for l in range(DEPTH) for u in ORDER]
        cv_emitted = set()

        def emit_cv(l, u):
            name, c0 = UNIT_SRC[u]
            src = wsrc[name]
            grp = cv_group(l, u)
            if name == "w_ple":
                in_ = src[l].rearrange("(k p) c -> p k c", p=128)
                out = wscr[l, u][:, 0:2 * D].rearrange("p (k c) -> p k c", k=2)
            else:
                in_ = src[l].rearrange("(k p) c -> p k c", p=128)[:, :, c0:c0 + 512]
                out = wscr[l, u].rearrange("p (k c) -> p k c", k=8)
            ev = S.dma("pool", ("cv", l, grp), out, in_)
            S.last_w[("scr", l, grp)] = ev
            cv_emitted.add((l, u))

        def pump(n):
            for _ in range(n):
                if pending_cv:
                    emit_cv(*pending_cv.pop(0))

        def ensure_cv(l, u):
            while (l, u) not in cv_emitted:
                pump(1)

        tile_layers = [(t, l) for t in range(4) for l in range(DEPTH)]
        stream = [(tl, u) for tl in range(len(tile_layers)) for u in ORDER]
        state = {"emitted": 0}
        done_set = set()

        def emit_stream():
            while state["emitted"] < len(stream) and (state["emitted"] < NSLOT or (state["emitted"] - NSLOT) in done_set):
                g = state["emitted"]
                tl, u = stream[g]
                l = tile_layers[tl][1]
                for uu in range(NUNITS):
                    if cv_group(l, uu) == cv_group(l, u):
                        ensure_cv(l, uu)
                slot = g % NSLOT
                width = 2 * D if u == NUNITS - 1 else UW
                S.dma("sp", ("ws", slot), wslots[slot][:, 0:width], wscr[l, u][:, 0:width],
                      reads=[("scr", l, cv_group(l, u))], writes=[("wslot", slot)])
                state["emitted"] += 1

        def unit(tl, u):
            g = tl * NUNITS + POS[u]
            assert g < state["emitted"], (tl, u)
            slot = g % NSLOT
            return wslots[slot], ("wslot", slot)

        def done(tl, *us):
            for u in us:
                done_set.add(tl * NUNITS + POS[u])
            emit_stream()

        def mm_group(out_ap, pairs, reads, bank_tok, first_start=True, last_stop=True, per_reads=None):
            def fn(eng):
                inst = None
                n = len(pairs)
                for i, (lt, rh) in enumerate(pairs):
                    if per_reads is not None:
                        S._waits("pe", per_reads[i], [])
                    inst = eng.matmul(out_ap, lhsT=lt, rhs=rh,
                                      start=(first_start and i == 0), stop=(last_stop and i == n - 1))
                return inst
            ev = S.op("pe", fn, reads=reads, writes=[bank_tok])
            if per_reads is not None:
                for rr in per_reads:
                    for r in rr:
                        S.readers.setdefault(r, []).append(ev)
            return ev

        HBK = [[("hb", k)] for k in range(8)]

        def transposes(items, reads, bank_toks):
            def fn(eng):
                inst = None
                for (o, i_, K) in items:
                    inst = eng.transpose(o, i_, ident_f[0:K, 0:K])
                return inst
            return S.op("pe", fn, reads=list(reads) + ["ident_f"], writes=list(bank_toks))

        def act(out, in_, func, reads, writes, bias=None, scale=None):
            kw = {}
            if bias is not None:
                kw["bias"] = bias
            if scale is not None:
                kw["scale"] = scale
            return S.op("act", lambda e: e.activation(out=out, in_=in_, func=func, **kw), reads=reads, writes=writes)

        def tt(e, out, in0, in1, op, reads, writes):
            return S.op(e, lambda g: g.tensor_tensor(out=out, in0=in0, in1=in1, op=op), reads=reads, writes=writes)

        def stt(out, in0, scalar, in1, op0, op1, reads, writes):
            return S.op("dve", lambda g: g.scalar_tensor_tensor(out=out, in0=in0, scalar=scalar, in1=in1, op0=op0, op1=op1),
                        reads=reads, writes=writes)

        def ts(e, out, in0, s1, s2, op0, op1, reads, writes):
            if op1 is None:
                return S.op(e, lambda g: g.tensor_scalar(out=out, in0=in0, scalar1=s1, scalar2=None, op0=op0),
                            reads=reads, writes=writes)
            return S.op(e, lambda g: g.tensor_scalar(out=out, in0=in0, scalar1=s1, scalar2=s2, op0=op0, op1=op1),
                        reads=reads, writes=writes)

        def copy(e, out, in_, reads, writes):
            if e == "act":
                return S.op("act", lambda g: g.copy(out=out, in_=in_), reads=reads, writes=writes)
            return S.op(e, lambda g: g.tensor_copy(out=out, in_=in_), reads=reads, writes=writes)

        def rstd_from(sum_ap, sum_reads, sum_writes, scale, Tn, dst=None):
            r1, r1t = ST.next()
            act(r1[:, 0:Tn], sum_ap, AF.Sqrt, reads=list(sum_reads) + ["epsc"], writes=[r1t] + list(sum_writes),
                bias=epsc[:, 0:1], scale=scale)
            r2, r2t = ST.next() if dst is None else dst
            S.op("dve", lambda g: g.reciprocal(out=r2[:, 0:Tn], in_=r1[:, 0:Tn]), reads=[r1t], writes=[r2t])
            return r2, r2t

        Aflat = A[:].rearrange("p a b -> p (a b)")

        S.op("pool", lambda g: g.memset(ones_f[:], 1.0), writes=["ones_f"])
        S.op("pool", lambda g: g.memset(ident_f[:], 0.0), writes=["ident_f"])
        S.op("pool", lambda g: g.affine_select(out=ident_f[:], in_=ones_f[:], pattern=[[1, 128]],
                                                compare_op=ALU.is_equal, fill=0.0, base=0, channel_multiplier=-1),
             reads=["ones_f"], writes=["ident_f"])
        S.op("pool", lambda g: g.tensor_copy(out=ident_b[:], in_=ident_f[:]), reads=["ident_f"], writes=["ident_b"])
        S.op("pool", lambda g: g.memset(ones_b[:], 1.0), writes=["ones_b"])
        S.op("pool", lambda g: g.memset(epsc[:], EPS), writes=["epsc"])
        S.op("pool", lambda g: g.memset(epsc2[:], 1.0), writes=["epsc2"])
        for i in range(2):
            S.op("pool", lambda g, i=i: g.memset(xab[i][:, 0:KC - 1], 0.0), writes=[("xab", i)])

        pump(NUNITS)


        VA = Aflat[:, 0:128]
        VB = Aflat[0:64, 128:256]
        vecsA = [norm_g, ple_norm_g, conv_b, ln_a_g]
        vecsB = [ln_a_b, b_ple_gate]
        for i, v in enumerate(vecsA):
            S.dma("sp", ("ld", "va", i), Aflat[32 * i:32 * i + 32, 0:128], v.rearrange("l (j c) -> (l j) c", c=128),
                  writes=[("VA", i)])
        for i, v in enumerate(vecsB):
            S.dma("sp", ("ld", "vb", i), Aflat[32 * i:32 * i + 32, 128:256], v.rearrange("l (j c) -> (l j) c", c=128),
                  writes=[("VA", 4 + i)])
        CW = Aflat[0:DEPTH * KC, 1024:2048]
        S.dma("sp", ("ld", "cw"), CW, conv_w.rearrange("l k c -> (l k) c"), writes=[("P", "A", 2), ("P", "A", 3)])
        with nc.allow_non_contiguous_dma(reason="tiny 32-element gather of w_spatial[:, :, 0, 0]"):
            S.dma("sp", ("ld", "wm00"), wm00[:], w_spatial.rearrange("l g t s -> (l g) (t s)")[:, 0].partition_broadcast(NS),
                  writes=["wm00"])

        bk, bkt = PS.next()
        transposes([(bk[:, 0:128], VA, 128), (bk[0:128, 128:192], VB, 64)],
                   reads=[("P", "A", 0)] + [("VA", i) for i in range(6)], bank_toks=[bkt])
        copy("dve", cols[:, 0:192], bk[:, 0:192], reads=[], writes=[bkt, "cols"])
        for half in range(2):
            bk, bkt = PS.next()
            transposes([(bk[:, jj * 124:(jj + 1) * 124], CW[:, (half * 4 + jj) * 128:(half * 4 + jj + 1) * 128], DEPTH * KC)
                        for jj in range(4)], reads=[("P", "A", 2), ("P", "A", 3)], bank_toks=[bkt])
            copy("dve", cwT[:, half * 4:half * 4 + 4, :], bk[:, 0:496].rearrange("p (j k) -> p j k", j=4),
                 reads=[], writes=[bkt, "cwT"])
        for l in range(DEPTH):
            WS = Aflat[:, 2048 + (l % 2) * 1024: 3072 + (l % 2) * 1024].rearrange("p (g s) -> p g s", g=8)
            wst = ("P", "A", 4 + 2 * (l % 2))
            wst2 = ("P", "A", 5 + 2 * (l % 2))
            S.dma("sp", ("ld", "ws", l % 2), WS, w_spatial[l].rearrange("g t s -> t g s"), writes=[wst, wst2])
            S.op("pool", lambda g, WS=WS: g.affine_select(out=WS, in_=WS, pattern=[[0, 8], [-1, 128]],
                                                          compare_op=ALU.is_ge, fill=0.0, base=0, channel_multiplier=1),
                 reads=[wst, wst2], writes=[wst, wst2])
            for half in range(2):
                bk, bkt = PS.next()
                transposes([(bk[:, jj * 128:(jj + 1) * 128], WS[:, half * 4 + jj, :], 128) for jj in range(4)],
                           reads=[wst, wst2], bank_toks=[bkt])
                copy("dve", wmT[:, l, half * 4:half * 4 + 4, :], bk[:].rearrange("p (g t) -> p g t", g=4),
                     reads=[], writes=[bkt, "wmT"])
        tt("dve", wm00d[:], ident_b[0:NS, 0:NS].unsqueeze(1).to_broadcast([NS, DEPTH * 8, NS]),
           wm00[:].unsqueeze(2).to_broadcast([NS, DEPTH * 8, NS]), ALU.mult, reads=["ident_b", "wm00"], writes=["wm00d"])

        class Stream:
            pass

        P = Stream()
        P.pfx, P.is_s, P.Tn, P.NB, P.nt = "P", False, TT, 4, 128
        P.xT, P.hb, P.A, P.pain, P.pbin, P.vnb = xT, hb, A, pain, pbin, vnb
        P.xaf, P.pin, P.pT, P.small = xaf, pin, pT, small

        SX = Stream()
        SX.pfx, SX.is_s, SX.Tn, SX.NB, SX.nt = "S", True, NS, 1, NS
        SX.xT = sb("xT_s", [128, 8, NS], F32)
        SX.hb = sb("hb_s", [128, 8, NS], BF16)
        SX.A = sb("A_s", [128, 8, NS], F32)
        SX.pain = sb("pain_s", [128, 8, NS], BF16)
        SX.pbin = sb("pbin_s", [128, 8, TT], BF16)
        SX.vnb = sb("vnb_s", [NS, 1, D], BF16)
        SX.xaf = sb("xaf_s", [128, 8, NS], F32)
        SX.pin = sb("pin_s", [NS, 1, PLE], F32)
        SX.pT = sb("pT_s", [128, 2, NS], BF16)
        SX.small = sb("small_s", [NS, 64], F32)
        SX.xin = ystage[0]
        SX.yab = [sb(f"yab_s{i}", [128, NS], BF16) for i in range(2)]
        SX.ysq = [sb(f"ysq_s{i}", [128, NS], BF16) for i in range(2)]
        SX.rAb = sb("rAb_s", [128, NS], F32)
        SX.nmrb = sb("nmrb_s", [128, NS], F32)
        P.yab, P.ysq, P.rAb, P.nmrb = yab, ysq, rAb, nmrb
        P.pz, SX.pz = {}, {}

        def pump_through(l1):
            while pending_cv and pending_cv[0][0] <= l1:
                emit_cv(*pending_cv.pop(0))

        def T(st, name, i=None):
            return (st.pfx, name, i)

        def xxs_j(j):
            return SX.pbin[:, j, NS:NS + KC * NS].rearrange("p (k s) -> p k s", s=NS)

        def build_diag(l_, j_):
            jb_ = j_ % 2
            S.op("pool", lambda g: g.tensor_tensor(
                out=diag[jb_][:], in0=ident_b[:].unsqueeze(1).to_broadcast([128, KC, 128]),
                in1=cwT[:, j_, l_ * KC:(l_ + 1) * KC].unsqueeze(2).to_broadcast([128, KC, 128]), op=ALU.mult),
                reads=["ident_b", "cwT"], writes=[("diag", jb_)])

        build_diag(0, 0)
        build_diag(0, 1)

        def preload(func):
            act(epsc2[:, 1:2], epsc2[:, 0:1], func, reads=[], writes=["epsc2"])

        def HBK(st):
            return [[T(st, "hb", k)] for k in range(8)]

        ystage_i = [0]
        tl_list = [(t, l) for t in range(4) for l in range(DEPTH)]

        def load_x(t):
            S.dma("sp", ("ld", "x"), Aflat.rearrange("p (b d) -> p b d", b=4),
                  xp[t * TT:(t + 1) * TT, :].rearrange("(b p) d -> p b d", p=128),
                  writes=[("P", "A", j) for j in range(8)])

        def tile_start(st, t):
            Tn, NB, nt = st.Tn, st.NB, st.nt
            if st.is_s:
                S.dma("sp", ("ld", "xs"), st.xin[0:NS, :], xs[:, :], writes=[("ystage", 0)])
                rd = [("ystage", 0)]
            else:
                if t == 0:
                    load_x(0)
                rd = [T(st, "A", j) for j in range(8)]
            for k in range(8):
                bk, bkt = PS.next()
                items = []
                for b in range(NB):
                    if st.is_s:
                        src = st.xin[0:nt, k * 128:(k + 1) * 128]
                    else:
                        src = Aflat[0:nt, b * D + k * 128: b * D + (k + 1) * 128]
                    items.append((bk[:, b * 128: b * 128 + nt], src, nt))
                transposes(items, reads=rd, bank_toks=[bkt])
                copy("act" if k % 2 else "dve", st.xT[:, k, 0:Tn], bk[:, 0:Tn], reads=[], writes=[bkt, T(st, "xT", k)])

        def layer_loads(st, t, l):
            if st.is_s:
                S.dma("sp", ("ld", "ps"), st.pin[0:NS, 0, :], psm[l], writes=[T(st, "pin")])
            else:
                S.dma("sp", ("ld", "p"), st.pin[:], pp[l, t * TT:(t + 1) * TT, :].rearrange("(b p) d -> p b d", p=128),
                      writes=[T(st, "pin")])

        SSTG = [(ystage[0], ("ystage", 0)), (ystage[1], ("ystage", 1))]

        def sample_state_dma(l, pieces):
            for g4 in pieces:
                buf, btok = SSTG[g4 % 2]
                S.dma("sp", ("ld", "sc", g4 % 2), buf[0:120, :], sc[l, g4 * 4:(g4 + 1) * 4].rearrange("s k d -> (s k) d"),
                      writes=[btok])

        def sample_state_tr(pieces):
            st = SX
            for g4 in pieces:
                buf, btok = SSTG[g4 % 2]
                for half in range(2):
                    bk, bkt = PS.next()
                    transposes([(bk[:, jj * 120:(jj + 1) * 120], buf[0:120, (half * 4 + jj) * 128:(half * 4 + jj + 1) * 128], 120)
                                for jj in range(4)], reads=[btok], bank_toks=[bkt])
                    outv = st.pbin[:, half * 4:half * 4 + 4, NS:NS + KC * NS].rearrange("p j (k s) -> p j k s", s=NS)
                    copy("act" if half else "dve",
                         outv[:, :, 0:KC - 1, g4 * 4:(g4 + 1) * 4],
                         bk[:, 0:480].rearrange("p (j s k) -> p j k s", j=4, s=4),
                         reads=[], writes=[bkt] + [T(st, "pbin", half * 4 + jj) for jj in range(4)])

        def stage0(st, l, gvec, first):
            Tn = st.Tn
            for k in range(8):
                act(st.pbin[:, k, 0:Tn], st.xT[:, k, 0:Tn], AF.Square, reads=[T(st, "xT", k)], writes=[T(st, "pbin", k)])
            bk, bkt = PS.next()
            mm_group(bk[:, 0:Tn], [(ones_b[:], st.pbin[:, k, 0:Tn]) for k in range(8)],
                     reads=["ones_b"], bank_tok=bkt, per_reads=[[T(st, "pbin", k)] for k in range(8)])
            rs, rst = rstd_from(bk[:, 0:Tn], [], [bkt], 1.0 / D, Tn)
            for k in range(8):
                stt(st.hb[:, k, 0:Tn], st.xT[:, k, 0:Tn], col(gvec, l, k), rs[:, 0:Tn], ALU.mult, ALU.mult,
                    reads=[T(st, "xT", k), rst, "cols"], writes=[T(st, "hb", k)])

        def p_transposes(st):
            Tn, NB, nt = st.Tn, st.NB, st.nt
            for c2 in range(2):
                bk, bkt = PS.next()
                transposes([(bk[:, b * 128: b * 128 + nt], st.pin[0:nt, b, c2 * 128:(c2 + 1) * 128], nt) for b in range(NB)],
                           reads=[T(st, "pin")], bank_toks=[bkt])
                copy("act", st.pT[:, c2, 0:Tn], bk[:, 0:Tn], reads=[], writes=[bkt, T(st, "pT", c2)])

        def proj_group(st, w, wt, jj, hold=False):
            Tn = st.Tn
            pb, pbt = PS.next(hold=hold)
            mm_group(pb[:, 0:Tn], [(w[:, k * 512 + jj * 128: k * 512 + (jj + 1) * 128], st.hb[:, k, 0:Tn]) for k in range(8)],
                     reads=[wt], bank_tok=pbt, per_reads=HBK(st))
            return pb, pbt

        def s1_proj(st, tl, t, l, j):
            Tn = st.Tn
            jj = j % 4
            wal, walt = unit(tl, 0 if j < 4 else 2)
            wag, wagt = unit(tl, 1 if j < 4 else 3)
            pa, pat = proj_group(st, wal, walt, jj)
            pg, pgt = proj_group(st, wag, wagt, jj)
            sg, sgt = TMP.next()
            act(sg[:, 0:Tn], pg[:, 0:Tn], AF.Sigmoid, reads=[], writes=[pgt, sgt])
            jb = j % 2
            if st.is_s:
                tt("dve", xxs_j(j)[:, KC - 1, :], pa[:, 0:Tn], sg[:, 0:Tn], ALU.mult, reads=[sgt], writes=[pat, T(st, "pbin", j)])
            else:
                if t > 0:
                    copy("pool", xab[jb][:, 0:KC - 1], hist[:, l, j, :], reads=[("hist", l, j)], writes=[("xab", jb)])
                tt("dve", xab[jb][:, KC - 1:KC - 1 + TT], pa[:, 0:TT], sg[:, 0:TT], ALU.mult,
                   reads=[sgt], writes=[pat, ("xab", jb)])
                if t < 3:
                    copy("pool", hist[:, l, j, :], xab[jb][:, TT:TT + KC - 1], reads=[("xab", jb)], writes=[("hist", l, j)])
            if st.is_s or t == 3:
                nrow = NS if st.is_s else KC - 1
                c0 = 0 if st.is_s else TT - (KC - 1)
                tt("dve", st.xaf[:, j, 0:nrow], pa[:, c0:c0 + nrow], sg[:, c0:c0 + nrow], ALU.mult,
                   reads=[sgt], writes=[pat, T(st, "xaf", j)])

        def s1_conv(st, l, j):
            Tn = st.Tn
            jb = j % 2
            py, pyt = PS.next()
            if st.is_s:
                pairs = [(diag[jb][:, k, :], xxs_j(j)[:, k, :]) for k in range(KC)]
                rd = [("diag", jb), T(st, "pbin", j)]
            else:
                pairs = [(diag[jb][:, k, :], xab[jb][:, k:k + TT]) for k in range(KC)]
                rd = [("diag", jb), ("xab", jb)]
            mm_group(py[:, 0:Tn], pairs, reads=rd, bank_tok=pyt)
            yq, yqt = st.ysq[jb], T(st, "ysq", jb)
            ya, yat = st.yab[jb], T(st, "yab", jb)
            act(st.A[:, j, 0:Tn], py[:, 0:Tn], AF.Identity, reads=["cols"], writes=[pyt, T(st, "A", j)], bias=col(V_CONVB, l, j))
            act(yq[:, 0:Tn], py[:, 0:Tn], AF.Square, reads=["cols"], writes=[pyt, yqt], bias=col(V_CONVB, l, j))
            copy("pool", ya[:, 0:Tn], st.A[:, j, 0:Tn], reads=[T(st, "A", j)], writes=[yat])

        def s1_stats(st, j):
            Tn = st.Tn
            jb = j % 2

            def fn(eng):
                if st.is_s:
                    eng.matmul(st.ssum, lhsT=ones_b[:], rhs=st.yab[jb][:, 0:Tn], start=(j == 0), stop=(j == 7))
                    return eng.matmul(st.ssq, lhsT=ones_b[:], rhs=st.ysq[jb][:, 0:Tn], start=False, stop=(j == 7),
                                      skip_group_check=True)
                eng.matmul(st.ssum, lhsT=ones_b[:], rhs=st.yab[jb][:, 0:Tn], start=(j == 0), stop=(j == 7))
                return eng.matmul(st.ssq, lhsT=ones_b[:], rhs=st.ysq[jb][:, 0:Tn], start=(j == 0), stop=(j == 7))
            S.op("pe", fn, reads=["ones_b", T(st, "yab", jb), T(st, "ysq", jb)], writes=list(st.stat_toks))

        def state_out(st, l):
            nrow = NS if st.is_s else KC - 1
            bk0, bk0t = PS.next()
            bk1, bk1t = PS.next()
            transposes([((bk0 if j < 4 else bk1)[0:nrow, (j % 4) * 128:(j % 4 + 1) * 128], st.xaf[:, j, 0:nrow], 128)
                        for j in range(8)], reads=[T(st, "xaf", j) for j in range(8)], bank_toks=[bk0t, bk1t])
            cstage, CST_T = ystage[1], ("ystage", 1)
            copy("act", cstage[0:nrow, 0:512], bk0[0:nrow, :], reads=[], writes=[bk0t, CST_T])
            copy("act", cstage[0:nrow, 512:1024], bk1[0:nrow, :], reads=[], writes=[bk1t, CST_T])
            if st.is_s:
                S.dma("pool", ("o", "y", 1), cs[l, :, KC - 2, :], cstage[0:nrow, :], reads=[CST_T], is_output=True)
            else:
                S.dma("pool", ("o", "y", 1), cp[l], cstage[0:nrow, :], reads=[CST_T], is_output=True)

        def stage3(st, tl, t, l):
            NB, nt = st.NB, st.nt
            wv0, wv0t = unit(tl, 4)
            wv1, wv1t = unit(tl, 5)
            sm, smt = st.small, T(st, "small")
            for b in range(NB):
                pv = []
                for half, (wv, wvt) in enumerate(((wv0, wv0t), (wv1, wv1t))):
                    bk, bkt = PS.next()
                    mm_group(bk[0:nt, :], [(st.hb[:, k, b * 128: b * 128 + nt], wv[:, k * 512:(k + 1) * 512]) for k in range(8)],
                             reads=[wvt], bank_tok=bkt, per_reads=HBK(st))
                    pv.append((bk, bkt))
                S.op("dve", lambda g, pv=pv: g.bn_stats(out=sm[0:nt, 0:6], in_=pv[0][0][0:nt, :]), reads=[], writes=[pv[0][1], smt])
                S.op("dve", lambda g, pv=pv: g.bn_stats(out=sm[0:nt, 6:12], in_=pv[1][0][0:nt, :]), reads=[], writes=[pv[1][1], smt])
                S.op("dve", lambda g: g.bn_aggr(out=sm[0:nt, 12:14], in_=sm[0:nt, 0:12]), reads=[smt], writes=[smt])
                act(sm[0:nt, 14:15], sm[0:nt, 13:14], AF.Sqrt, reads=[smt, "epsc"], writes=[smt], bias=epsc[0:nt, 0:1], scale=1.0)
                S.op("dve", lambda g: g.reciprocal(out=sm[0:nt, 15:16], in_=sm[0:nt, 14:15]), reads=[smt], writes=[smt])
                stt(sm[0:nt, 16:17], sm[0:nt, 12:13], -1.0, sm[0:nt, 15:16], ALU.mult, ALU.mult, reads=[smt], writes=[smt])
                for half in range(2):
                    act(tmpv[0:nt, half * 512:(half + 1) * 512], pv[half][0][0:nt, :], AF.Identity,
                        reads=[smt], writes=[pv[half][1], "tmpv"], bias=sm[0:nt, 16:17], scale=sm[0:nt, 15:16])
                tt("pool", tmpv[0:nt, :], tmpv[0:nt, :], lnvg[0:nt, :], ALU.mult, reads=["lnvg"], writes=["tmpv"])
                tt("pool", st.vnb[0:nt, b, :], tmpv[0:nt, :], lnvb[0:nt, :], ALU.add, reads=["tmpv", "lnvb"], writes=[T(st, "vnb", b)])
                if st.is_s or (t == 3 and b == NB - 1):
                    vnf, VNF_T = ystage[0], ("ystage", 0)
                    tt("pool", vnf[0:nt, :], tmpv[0:nt, :], lnvb[0:nt, :], ALU.add, reads=["tmpv", "lnvb"], writes=[VNF_T])
                    S.dma("pool", ("o", "y", 0), vs[l] if st.is_s else vp[l], vnf[0:nt, :], reads=[VNF_T], is_output=True)

        def stage4(st):
            Tn = st.Tn
            mean, meant = ST.next()
            ts("dve", mean[:, 0:Tn], st.ssum, 1.0 / D, None, ALU.mult, None, reads=[], writes=list(st.stat_toks) + [meant])
            msq, msqt = ST.next()
            tt("dve", msq[:, 0:Tn], mean[:, 0:Tn], mean[:, 0:Tn], ALU.mult, reads=[meant], writes=[msqt])
            var, vart = ST.next()
            stt(var[:, 0:Tn], st.ssq, 1.0 / D, msq[:, 0:Tn], ALU.mult, ALU.subtract, reads=[msqt], writes=list(st.stat_toks) + [vart])
            for tok in st.stat_toks:
                PS.release(tok)
            rA, rAt = rstd_from(var[:, 0:Tn], [vart], [], 1.0, Tn, dst=(st.rAb, T(st, "rAb")))
            nmr, nmrt = st.nmrb, T(st, "nmrb")
            stt(nmr[:, 0:Tn], mean[:, 0:Tn], -1.0, rA[:, 0:Tn], ALU.mult, ALU.mult, reads=[meant, rAt], writes=[nmrt])
            st.rA, st.rAt, st.nmr, st.nmrt = rA, rAt, nmr, nmrt

        def s5_pe(st, tl, j, hold=False):
            jj = j % 4
            wz, wzt = unit(tl, 6 if j < 4 else 7)
            st.pz[j] = proj_group(st, wz, wzt, jj, hold=hold)

        def s5_chunk(st, tl, l, j):
            Tn = st.Tn
            if j not in st.pz:
                s5_pe(st, tl, j)
            pz, pzt = st.pz.pop(j)
            PS.release(pzt)
            sz, szt = TMP.next()
            act(sz[:, 0:Tn], pz[:, 0:Tn], AF.Silu, reads=[], writes=[pzt, szt])
            t1, t1t = TMP.next()
            tt("pool", t1[:, 0:Tn], st.A[:, j, 0:Tn], st.rA[:, 0:Tn], ALU.mult, reads=[T(st, "A", j), st.rAt], writes=[t1t])
            tt("dve", t1[:, 0:Tn], t1[:, 0:Tn], st.nmr[:, 0:Tn], ALU.add, reads=[t1t, st.nmrt], writes=[t1t])
            act(t1[:, 0:Tn], t1[:, 0:Tn], AF.Silu, reads=[t1t, "cols"], writes=[t1t],
                bias=col(V_LNAB, l, j), scale=col(V_LNAG, l, j))
            tt("dve", st.pain[:, j, 0:Tn], t1[:, 0:Tn], sz[:, 0:Tn], ALU.mult, reads=[t1t, szt], writes=[T(st, "pain", j)])

        def s6_group(st, tl, l, g):
            Tn, NB, nt = st.Tn, st.NB, st.nt
            gg = g % 4
            wu, wut = unit(tl, 8 if g < 4 else 10)
            wzb, wzbt = unit(tl, 9 if g < 4 else 11)
            psx, psxt = PS.next()

            def fn(eng):
                inst = None
                for b in range(NB):
                    o = psx[:, b * 128: b * 128 + nt]
                    if st.is_s:
                        rhs = wm00d[0:NS, l * 8 + g, :]
                        brow = bsps[0:1, g, :]
                    else:
                        rhs = wmT[:, l, g, :]
                        brow = bsp[0:1, g * 128:(g + 1) * 128]
                    eng.matmul(o, lhsT=st.vnb[0:nt, b, g * 128:(g + 1) * 128], rhs=rhs, start=True, stop=False)
                    inst = eng.matmul(o, lhsT=ones_b[0:1, :], rhs=brow, start=False, stop=True)
                return inst
            S.op("pe", fn, reads=[T(st, "vnb", b) for b in range(NB)] + ["wmT", "bsp", "bsps", "wm00d", "ones_b"], writes=[psxt])
            pu, put = proj_group(st, wu, wut, gg)
            pzb, pzbt = proj_group(st, wzb, wzbt, gg)
            szb, szbt = TMP.next()
            act(szb[:, 0:Tn], pzb[:, 0:Tn], AF.Silu, reads=[], writes=[pzbt, szbt])
            t1, t1t = TMP.next()
            tt("dve", t1[:, 0:Tn], pu[:, 0:Tn], szb[:, 0:Tn], ALU.mult, reads=[szbt], writes=[put, t1t])
            tt("dve", st.pbin[:, g, 0:Tn], psx[:, 0:Tn], t1[:, 0:Tn], ALU.mult, reads=[t1t], writes=[psxt, T(st, "pbin", g)])

        def in_group(st, w, wt, jj, src, srcname):
            Tn = st.Tn
            pb, pbt = PS.next()
            mm_group(pb[:, 0:Tn], [(w[:, k * 512 + jj * 128: k * 512 + (jj + 1) * 128], src[:, k, 0:Tn]) for k in range(8)],
                     reads=[wt], bank_tok=pbt, per_reads=[[T(st, srcname, k)] for k in range(8)])
            return pb, pbt

        def s7_chunk(st, tl, j):
            Tn = st.Tn
            jj = j % 4
            wpa, wpat = unit(tl, 12 if j < 4 else 14)
            wga, wgat = unit(tl, 13 if j < 4 else 15)
            pya, pyat = in_group(st, wpa, wpat, jj, st.pain, "pain")
            pga, pgat = proj_group(st, wga, wgat, jj)
            sga, sgat = TMP.next()
            act(sga[:, 0:Tn], pga[:, 0:Tn], AF.Sigmoid, reads=[], writes=[pgat, sgat])
            tt("dve", st.A[:, j, 0:Tn], pya[:, 0:Tn], sga[:, 0:Tn], ALU.mult, reads=[sgat], writes=[pyat, T(st, "A", j)])

        def s8_chunk(st, tl, j):
            Tn = st.Tn
            jj = j % 4
            wpb, wpbt = unit(tl, 16 if j < 4 else 18)
            wgb, wgbt = unit(tl, 17 if j < 4 else 19)
            pyb, pybt = in_group(st, wpb, wpbt, jj, st.pbin, "pbin")
            pgb, pgbt = proj_group(st, wgb, wgbt, jj)
            sgb, sgbt = TMP.next()
            act(sgb[:, 0:Tn], pgb[:, 0:Tn], AF.Sigmoid, reads=[], writes=[pgbt, sgbt])
            t1, t1t = TMP.next()
            tt("dve", t1[:, 0:Tn], pyb[:, 0:Tn], sgb[:, 0:Tn], ALU.mult, reads=[sgbt], writes=[pybt, t1t])
            tt("dve" if j >= 6 else "pool", st.pain[:, j, 0:Tn], st.A[:, j, 0:Tn], t1[:, 0:Tn], ALU.add,
               reads=[T(st, "A", j), t1t], writes=[T(st, "pain", j)])

        def s9_chunk(st, tl, j):
            Tn = st.Tn
            jj = j % 4
            wo, wot = unit(tl, 20 if j < 4 else 21)
            po, pot = in_group(st, wo, wot, jj, st.pain, "pain")
            tt("dve", st.xT[:, j, 0:Tn], st.xT[:, j, 0:Tn], po[:, 0:Tn], ALU.add, reads=[T(st, "xT", j)], writes=[pot, T(st, "xT", j)])

        def s10_pw(st, tl):
            Tn = st.Tn
            wpl, wplt = unit(tl, 24)
            for j in range(8):
                ppw, ppwt = PS.next()
                mm_group(ppw[:, 0:Tn], [(wpl[:, c2 * D + j * 128: c2 * D + (j + 1) * 128], st.pT[:, c2, 0:Tn]) for c2 in range(2)],
                         reads=[T(st, "pT", 0), T(st, "pT", 1), wplt], bank_tok=ppwt)
                copy("act", st.A[:, j, 0:Tn], ppw[:, 0:Tn], reads=[], writes=[ppwt, T(st, "A", j)])

        def s10_chunk(st, tl, l, j):
            Tn = st.Tn
            jj = j % 4
            wpg, wpgt = unit(tl, 22 if j < 4 else 23)
            pgt_, pgtt = proj_group(st, wpg, wpgt, jj)
            gp, gpt = TMP.next()
            act(gp[:, 0:Tn], pgt_[:, 0:Tn], AF.Sigmoid, reads=["cols"], writes=[pgtt, gpt], bias=col(V_BPG, l, j))
            t1, t1t = TMP.next()
            tt("dve", t1[:, 0:Tn], st.A[:, j, 0:Tn], gp[:, 0:Tn], ALU.mult, reads=[gpt, T(st, "A", j)], writes=[t1t])
            tt("dve" if j >= 6 else "pool", st.xT[:, j, 0:Tn], st.xT[:, j, 0:Tn], t1[:, 0:Tn], ALU.add,
               reads=[T(st, "xT", j), t1t], writes=[T(st, "xT", j)])

        def final_out(st, t):
            NB, nt = st.NB, st.nt
            sm, smt = st.small, T(st, "small")
            for b in range(NB):
                bk0, bk0t = PS.next()
                bk1, bk1t = PS.next()
                transposes([((bk0 if k < 4 else bk1)[0:nt, (k % 4) * 128:(k % 4 + 1) * 128], st.xT[:, k, b * 128: b * 128 + nt], 128)
                            for k in range(8)], reads=[T(st, "xT", k) for k in range(8)], bank_toks=[bk0t, bk1t])
                S.op("dve", lambda g: g.bn_stats(out=sm[0:nt, 0:6], in_=bk0[0:nt, :]), reads=[], writes=[bk0t, smt])
                S.op("dve", lambda g: g.bn_stats(out=sm[0:nt, 6:12], in_=bk1[0:nt, :]), reads=[], writes=[bk1t, smt])
                S.op("dve", lambda g: g.bn_aggr(out=sm[0:nt, 12:14], in_=sm[0:nt, 0:12]), reads=[smt], writes=[smt])
                stt(sm[0:nt, 14:15], sm[0:nt, 12:13], sm[0:nt, 12:13], sm[0:nt, 13:14], ALU.mult, ALU.add,
                    reads=[smt], writes=[smt])
                act(sm[0:nt, 15:16], sm[0:nt, 14:15], AF.Sqrt, reads=[smt, "epsc"], writes=[smt], bias=epsc[0:nt, 0:1], scale=1.0)
                S.op("dve", lambda g: g.reciprocal(out=sm[0:nt, 16:17], in_=sm[0:nt, 15:16]), reads=[smt], writes=[smt])
                yi = ystage_i[0] % 2
                ystage_i[0] += 1
                yst = ystage[yi]
                for half, (bkx, bkxt) in enumerate(((bk0, bk0t), (bk1, bk1t))):
                    stt(yst[0:nt, half * 512:(half + 1) * 512], bkx[0:nt, :], sm[0:nt, 16:17], lnvg[0:nt, half * 512:(half + 1) * 512],
                        ALU.mult, ALU.mult, reads=[smt, "lnvg"], writes=[bkxt, ("ystage", yi)])
                if st.is_s:
                    S.dma("pool", ("o", "y", yi), ys[:, :], yst[0:nt, :], reads=[("ystage", yi)], is_output=True)
                else:
                    r0 = t * TT + b * 128
                    S.dma("pool", ("o", "y", yi), yp[r0:r0 + 128, :], yst[0:nt, :], reads=[("ystage", yi)], is_output=True)

        RIDER_TILE = 1
        for tl, (t, l) in enumerate(tl_list):
            streams = [P] + ([SX] if t == RIDER_TILE else [])
            nxt = tl_list[tl + 1] if tl + 1 < len(tl_list) else None
            pre_state = nxt is not None and nxt[0] == RIDER_TILE
            cvp = (t == 0 and l < DEPTH - 1)

            if l == 0:
                for st in streams:
                    tile_start(st, t)
            S.dma("sp", ("ld", "lnvg"), lnvg[:], ln_v_g[l].partition_broadcast(128), writes=["lnvg"])
            S.dma("sp", ("ld", "lnvb"), lnvb[:], ln_v_b[l].partition_broadcast(128), writes=["lnvb"])
            S.dma("pool", ("ld", "bsp"), bsp[:], b_spatial[l].rearrange("g t -> (g t)").unsqueeze(0), writes=["bsp"])
            if SX in streams:
                copy("dve", bsps[:], bsp[:].rearrange("o (g t) -> o g t", t=128)[:, :, 0:1].to_broadcast([1, 8, NS]),
                     reads=["bsp"], writes=["bsps"])
            for st in streams:
                layer_loads(st, t, l)
            emit_stream()
            if 8 <= tl < 12:
                S.dma("sp", ("o", "csrows"), cs[tl - 8, :, 0:KC - 2, :], sc[tl - 8, :, 1:KC - 1, :], is_output=True)

            preload(AF.Sqrt)
            for st in streams:
                stage0(st, l, V_NORMG, True)
            preload(AF.Sigmoid)
            for st in streams:
                p_transposes(st)

            for st in streams:
                if st.is_s:
                    bk, bkt = PS.next(hold=True)
                    st.ssum, st.ssq, st.stat_toks = bk[:, 0:NS], bk[:, NS:2 * NS], [bkt]
                else:
                    b1, b1t = PS.next(hold=True)
                    b2, b2t = PS.next(hold=True)
                    st.ssum, st.ssq, st.stat_toks = b1[:, 0:TT], b2[:, 0:TT], [b1t, b2t]
            for step in range(10):
                if step < 8:
                    for st in streams:
                        s1_proj(st, tl, t, l, step)
                    if step == 3:
                        done(tl, 0, 1)
                    if step == 7:
                        done(tl, 2, 3)
                        preload(AF.Sqrt)
                    if cvp:
                        pump(1)
                if 1 <= step <= 8:
                    for st in streams:
                        s1_conv(st, l, step - 1)
                if 1 <= step <= 6:
                    build_diag(l, step + 1)
                if step == 9:
                    s5_pe(P, tl, 0, hold=True)
                    s5_pe(P, tl, 1, hold=True)
                if 2 <= step <= 9:
                    for st in streams:
                        s1_stats(st, step - 2)
            for st in streams:
                if st.is_s or t == 3:
                    state_out(st, l)

            for st in streams:
                stage4(st)
            preload(AF.Silu)
            for j in range(8):
                for st in streams:
                    s5_chunk(st, tl, l, j)
                if cvp:
                    pump(1)
                if j == 3:
                    done(tl, 6)
                if j == 7:
                    done(tl, 7)
            preload(AF.Sqrt)
            for st in streams:
                stage3(st, tl, t, l)
                if cvp:
                    pump(2)
            done(tl, 4, 5)
            if l == DEPTH - 1:
                S.dma("sp", ("ld", "lnvg"), lnvg[:], final_g.partition_broadcast(128), writes=["lnvg"])
            if pre_state:
                sample_state_dma(nxt[1], (0, 1))

            preload(AF.Sigmoid)
            if tl + 1 < len(tl_list):
                build_diag(tl_list[tl + 1][1], 0)
                build_diag(tl_list[tl + 1][1], 1)
            for j in range(8):
                for st in streams:
                    s7_chunk(st, tl, j)
                if j == 3:
                    done(tl, 12, 13)
                if j == 7:
                    done(tl, 14, 15)
            if pre_state:
                sample_state_tr((0, 1))
                sample_state_dma(nxt[1], (2, 3))
            preload(AF.Silu)
            for g in range(8):
                for st in streams:
                    s6_group(st, tl, l, g)
                if cvp:
                    pump(1)
                if g == 3:
                    done(tl, 8, 9)
                if g == 7:
                    done(tl, 10, 11)
                    preload(AF.Sigmoid)
            if cvp:
                pump_through(l + 1)
            if pre_state:
                sample_state_tr((2, 3))
            for j in range(8):
                for st in streams:
                    s8_chunk(st, tl, j)
                if j == 3:
                    done(tl, 16, 17)
                if j == 7:
                    done(tl, 18, 19)
            for j in range(8):
                for st in streams:
                    s9_chunk(st, tl, j)
                if j == 3:
                    done(tl, 20)
                if j == 7:
                    done(tl, 21)
            preload(AF.Sqrt)
            for st in streams:
                stage0(st, l, V_PLEG, False)
            for st in streams:
                s10_pw(st, tl)
            done(tl, 24)
            preload(AF.Sigmoid)
            for j in range(8):
                for st in streams:
                    s10_chunk(st, tl, l, j)
                if j == 3:
                    done(tl, 22)
                if j == 7:
                    done(tl, 23)
            if l == DEPTH - 1:
                if t + 1 < 4:
                    load_x(t + 1)
                for st in streams:
                    final_out(st, t)

        S.finish()
    return nc


_NC_CACHE = {}


def kernel(**inputs):
    inp = {k: np.ascontiguousarray(np.asarray(v)) for k, v in inputs.items()}
    if "nc" not in _NC_CACHE:
        _NC_CACHE["nc"] = build_program()
    nc = _NC_CACHE["nc"]
    wnames = ["norm_g", "w_in", "conv_w", "conv_b", "ln_a_g", "ln_a_b", "w_proj_a", "ln_v_g", "ln_v_b",
              "w_spatial", "b_spatial", "w_proj_b", "w_out", "ple_norm_g", "w_ple_gate", "b_ple_gate",
              "w_ple", "final_g"]
    in_maps = []
    for c in range(NCORES):
        m = {
            "xp": inp["x_prompt"][c],
            "xs": np.ascontiguousarray(inp["x_sample"][c * NS:(c + 1) * NS, 0, :]),
            "sc": np.ascontiguousarray(inp["state_conv"][:, c * NS:(c + 1) * NS]),
            "pp": np.ascontiguousarray(inp["p_prompt"][:, c]),
            "psm": np.ascontiguousarray(inp["p_sample"][:, c * NS:(c + 1) * NS, 0, :]),
        }
        for w in wnames:
            m[w] = inp[w]
        in_maps.append(m)
    res = run_bass_kernel_spmd(nc, in_maps, core_ids=list(range(NCORES)))
    R = res.results
    y_prompt = np.stack([R[c]["yp"] for c in range(NCORES)], axis=0)
    y_sample = np.concatenate([R[c]["ys"] for c in range(NCORES)], axis=0)[:, None, :]
    conv_prompt = np.stack([R[c]["cp"] for c in range(NCORES)], axis=1)
    conv_sample = np.concatenate([R[c]["cs"] for c in range(NCORES)], axis=1)
    vrows_prompt = np.stack([R[c]["vp"] for c in range(NCORES)], axis=1)
    vrows_sample = np.concatenate([R[c]["vs"] for c in range(NCORES)], axis=1)[:, :, None, :]
    return (y_prompt.astype(np.float32), y_sample.astype(np.float32), conv_prompt.astype(np.float32),
            conv_sample.astype(np.float32), vrows_prompt.astype(np.float32), vrows_sample.astype(np.float32))
```
